# Optimizing a Trainium2 kernel written in Bass

```python
import math
import jax, jax.numpy as jnp
from jax import lax
import numpy as np

D_MODEL = 2048
BATCH = 2
SEQ = 4096
DEPTH = 2
DEC_BATCH = 2
DEC_SEQ = 16384
PAST_LEN = 128

CONV_CH = D_MODEL // 4
CONV_WIDTH = 3
ATT_HD = 64
ATT_VD = 2 * ATT_HD
ATT_WIDTH = D_MODEL // 2
ATT_HEADS = ATT_WIDTH // ATT_VD
ATT_QK = ATT_HEADS * 2 * ATT_HD
ROT_DIM = ATT_HD // 4
ROPE_THETA = 500000.0
Q_BLOCK = 128
MLP_WIDTH = D_MODEL // 4
CHUNK = 128
MLP_HD = 128
MLP_HEADS = MLP_WIDTH // MLP_HD
D_FF = 4 * D_MODEL
EPS = 1e-6
SPLIT_SIZES = (CONV_CH, CONV_CH, CONV_CH, ATT_QK, ATT_QK, ATT_WIDTH, MLP_WIDTH, MLP_WIDTH)
IN_WIDTH = 3 * CONV_CH + 2 * ATT_QK + ATT_WIDTH + 2 * MLP_WIDTH
MIX_WIDTH = CONV_CH + ATT_WIDTH + MLP_WIDTH

kernel_name = "hymba_conv_diffattn_gmlp_encoder"


def rms_norm(x, g):
    xf = x.astype(jnp.float32)
    y = xf * lax.rsqrt(jnp.mean(xf * xf, axis=-1, keepdims=True) + EPS)
    return (y * g.astype(jnp.float32)).astype(x.dtype)


def rope_tables(s):
    inv = ROPE_THETA ** (-jnp.arange(0, ROT_DIM, 2, dtype=jnp.float32) / ROT_DIM)
    ang = jnp.arange(s, dtype=jnp.float32)[:, None] * inv[None, :]
    return jnp.cos(ang), jnp.sin(ang)


def apply_partial_rope(x, cos, sin):
    half = ROT_DIM // 2
    xf = x.astype(jnp.float32)
    x1 = xf[..., :half]
    x2 = xf[..., half:ROT_DIM]
    c = cos[None, :, None, None, :]
    s = sin[None, :, None, None, :]
    out = jnp.concatenate([x1 * c - x2 * s, x2 * c + x1 * s, xf[..., ROT_DIM:]], axis=-1)
    return out.astype(x.dtype)


def short_conv_mixer(xa, gb, gc, w_conv):
    z = gc * xa
    zp = jnp.pad(z, ((0, 0), (1, 1), (0, 0)))
    conv = w_conv[0] * zp[:, :-2] + w_conv[1] * zp[:, 1:-1] + w_conv[2] * zp[:, 2:]
    return gb * conv


def diff_attention(q, k, v, lam, g_subln, lambda_init):
    b, s = q.shape[0], q.shape[1]
    nb = s // Q_BLOCK
    qb = q.reshape(b, nb, Q_BLOCK, ATT_HEADS, 2, ATT_HD).swapaxes(0, 1)
    scale = ATT_HD ** -0.5

    def block(qblk):
        sc = jnp.einsum('bqhcd,bkhcd->bhcqk', qblk, k).astype(jnp.float32) * scale
        p = jax.nn.softmax(sc, axis=-1)
        a = p[:, :, 0] - lam * p[:, :, 1]
        return jnp.einsum('bhqk,bkhe->bqhe', a.astype(v.dtype), v)

    o = lax.map(block, qb)
    o = o.swapaxes(0, 1).reshape(b, s, ATT_HEADS, ATT_VD)
    o = rms_norm(o, g_subln) * (1.0 - lambda_init)
    return o.reshape(b, s, ATT_WIDTH)


def chunk_spatial_gating(u, v, w_s, b_s, g_v):
    b, s, _ = u.shape
    n = s // CHUNK
    vn = rms_norm(v.reshape(b, n, CHUNK, MLP_HEADS, MLP_HD), g_v)
    mixed = jnp.einsum('hqp,bnphd->bnqhd', w_s, vn) + b_s.T[None, None, :, :, None]
    return u * mixed.reshape(b, s, MLP_WIDTH)


def encoder_layer(x, l, cos, sin, norm1_g, w_in, conv_w, q_norm_g, k_norm_g,
                  lam_q1, lam_k1, lam_q2, lam_k2, subln_g, sgu_norm_g, sgu_w, sgu_b,
                  w_out, norm2_g, w_up, w_down):
    b, s, _ = x.shape
    h = rms_norm(x, norm1_g[l])
    proj = h @ w_in[l]
    points = []
    acc = 0
    for sz in SPLIT_SIZES[:-1]:
        acc += sz
        points.append(acc)
    a_x, a_b, a_c, q, k, v, c_u, c_v = jnp.split(proj, points, axis=-1)

    out_a = short_conv_mixer(a_x, a_b, a_c, conv_w[l])

    q = q.reshape(b, s, ATT_HEADS, 2, ATT_HD)
    k = k.reshape(b, s, ATT_HEADS, 2, ATT_HD)
    v = v.reshape(b, s, ATT_HEADS, ATT_VD)
    q = apply_partial_rope(rms_norm(q, q_norm_g[l]), cos, sin)
    k = apply_partial_rope(rms_norm(k, k_norm_g[l]), cos, sin)
    lambda_init = 0.8 - 0.6 * math.exp(-0.3 * l)
    lam = (jnp.exp(jnp.sum(lam_q1[l].astype(jnp.float32) * lam_k1[l].astype(jnp.float32)))
           - jnp.exp(jnp.sum(lam_q2[l].astype(jnp.float32) * lam_k2[l].astype(jnp.float32)))
           + lambda_init)
    out_b = diff_attention(q, k, v, lam, subln_g[l], lambda_init)

    out_c = chunk_spatial_gating(jax.nn.gelu(c_u), jax.nn.gelu(c_v), sgu_w[l], sgu_b[l], sgu_norm_g[l])

    mixed = jnp.concatenate([out_a, out_b, out_c], axis=-1)
    x = x + mixed @ w_out[l]

    h2 = rms_norm(x, norm2_g[l])
    x = x + jnp.square(jax.nn.relu(h2 @ w_up[l])) @ w_down[l]
    return x


def run_trunk(x, norm1_g, w_in, conv_w, q_norm_g, k_norm_g, lam_q1, lam_k1, lam_q2, lam_k2,
              subln_g, sgu_norm_g, sgu_w, sgu_b, w_out, norm2_g, w_up, w_down):
    cos, sin = rope_tables(x.shape[1])
    for l in range(DEPTH):
        x = encoder_layer(x, l, cos, sin, norm1_g, w_in, conv_w, q_norm_g, k_norm_g,
                          lam_q1, lam_k1, lam_q2, lam_k2, subln_g, sgu_norm_g, sgu_w, sgu_b,
                          w_out, norm2_g, w_up, w_down)
    return x


def setup_inputs(seed: int = 0) -> dict:
    key = jax.random.key(seed)
    ks = jax.random.split(key, 20)
    f32 = jnp.float32

    def nrm(k, shape, scale):
        return jax.random.normal(k, shape, f32) * scale

    def gain(k, shape):
        return 1.0 + 0.02 * jax.random.normal(k, shape, f32)

    return {
        "x_prompt": nrm(ks[0], (BATCH, SEQ, D_MODEL), 1.0),
        "x_sample": nrm(ks[1], (DEC_BATCH, DEC_SEQ, D_MODEL), 1.0),
        "norm1_g": gain(ks[2], (DEPTH, D_MODEL)),
        "w_in": nrm(ks[3], (DEPTH, D_MODEL, IN_WIDTH), D_MODEL ** -0.5),
        "conv_w": nrm(ks[4], (DEPTH, CONV_WIDTH, CONV_CH), CONV_WIDTH ** -0.5),
        "q_norm_g": gain(ks[5], (DEPTH, ATT_HD)),
        "k_norm_g": gain(ks[6], (DEPTH, ATT_HD)),
        "lam_q1": nrm(ks[7], (DEPTH, ATT_HD), 0.1),
        "lam_k1": nrm(ks[8], (DEPTH, ATT_HD), 0.1),
        "lam_q2": nrm(ks[9], (DEPTH, ATT_HD), 0.1),
        "lam_k2": nrm(ks[10], (DEPTH, ATT_HD), 0.1),
        "subln_g": gain(ks[11], (DEPTH, ATT_VD)),
        "sgu_norm_g": gain(ks[12], (DEPTH, MLP_HD)),
        "sgu_w": nrm(ks[13], (DEPTH, MLP_HEADS, CHUNK, CHUNK), CHUNK ** -0.5),
        "sgu_b": 1.0 + nrm(ks[14], (DEPTH, MLP_HEADS, CHUNK), 0.01),
        "w_out": nrm(ks[15], (DEPTH, MIX_WIDTH, D_MODEL), MIX_WIDTH ** -0.5),
        "norm2_g": gain(ks[16], (DEPTH, D_MODEL)),
        "w_up": nrm(ks[17], (DEPTH, D_MODEL, D_FF), D_MODEL ** -0.5),
        "w_down": nrm(ks[18], (DEPTH, D_FF, D_MODEL), D_FF ** -0.5),
    }


def reference(x_prompt, x_sample, norm1_g, w_in, conv_w, q_norm_g, k_norm_g, lam_q1, lam_k1,
              lam_q2, lam_k2, subln_g, sgu_norm_g, sgu_w, sgu_b, w_out, norm2_g, w_up, w_down):
    y_prompt = run_trunk(x_prompt, norm1_g, w_in, conv_w, q_norm_g, k_norm_g, lam_q1, lam_k1,
                         lam_q2, lam_k2, subln_g, sgu_norm_g, sgu_w, sgu_b, w_out, norm2_g,
                         w_up, w_down)
    y_sample = run_trunk(x_sample, norm1_g, w_in, conv_w, q_norm_g, k_norm_g, lam_q1, lam_k1,
                         lam_q2, lam_k2, subln_g, sgu_norm_g, sgu_w, sgu_b, w_out, norm2_g,
                         w_up, w_down)
    return (y_prompt, y_sample)
```

```python
import math
from contextlib import ExitStack

import numpy as np
import concourse.bass as bass
import concourse.mybir as mybir
from concourse.bass_utils import run_bass_kernel_spmd

F32 = mybir.dt.float32
BF16 = mybir.dt.bfloat16
U8 = mybir.dt.uint8
AF = mybir.ActivationFunctionType
ALU = mybir.AluOpType
AX = mybir.AxisListType

D = 2048
DIN = 5632
DFF = 8192
NH = 8
EPS = 1e-6
ROPE_THETA = 500000.0
NCORES = 8
GROUPS = [[0, 1, 2, 3], [4, 5, 6, 7]]
ENG = ("pe", "act", "dve", "pool", "sp")
ARENA_BYTES = 206 * 1024
SEM_ROT = 30000
EMBED_WAIT = False
SKIP_SAME_ENGINE = False


class Slot:
    def __init__(self, sem):
        self.sem = sem
        self.cnt = 0


class Sched:
    def __init__(self, nc, stack):
        self.nc = nc
        self.stack = stack
        self.q = {e: [] for e in ENG}
        self.sem = {}
        self.cnt = {}
        self.nsem = 0
        self.waited = {e: {} for e in ENG}
        self.last = {e: None for e in ENG}
        self.lw = {}
        self.lr = {}
        self.pend = {e: ([], []) for e in ENG}
        self.slots = {}
        self.owner = {}
        self.n_instr = 0
        for e in ENG:
            self._rot(e)

    def new_sem(self, name):
        self.nsem += 1
        return self.stack.enter_context(self.nc.semaphore(f"{name}{self.nsem}"))

    def _rot(self, e):
        self.sem[e] = self.new_sem("s" + e)
        self.owner[id(self.sem[e])] = e
        self.cnt[e] = 0

    def _need(self, eng, tok, out):
        if tok is None:
            return
        sem, val = tok
        w = self.waited[eng]
        if w.get(id(sem), 0) >= val:
            return
        w[id(sem)] = val
        out[id(sem)] = (sem, val)

    def _wait(self, eng, tok):
        out = {}
        self._need(eng, tok, out)
        for sem, val in out.values():
            self.q[eng].append(lambda e, sem=sem, val=val: e.wait_ge(sem, val))

    def _deps(self, eng, r, w):
        out = {}
        own = self.owner
        for k in r:
            self._need(eng, self.lw.get(k), out)
        for k in w:
            t = self.lw.get(k)
            if t is not None and (not SKIP_SAME_ENGINE or own.get(id(t[0])) != eng):
                self._need(eng, t, out)
            for t in self.lr.get(k, ()):
                if not SKIP_SAME_ENGINE or own.get(id(t[0])) != eng:
                    self._need(eng, t, out)
        return list(out.values())

    def need(self, eng, keys):
        for sem, val in self._deps(eng, keys, ()):
            self.q[eng].append(lambda e, sem=sem, val=val: e.wait_ge(sem, val))

    def _reg(self, tok, r, w):
        for k in r:
            self.lr.setdefault(k, []).append(tok)
        for k in w:
            self.lw[k] = tok
            self.lr[k] = []

    def _emit(self, eng, fn, toks, emb, inc):
        emb_tok = toks.pop() if (emb and EMBED_WAIT and toks) else None
        for sem, val in toks:
            self.q[eng].append(lambda e, sem=sem, val=val: e.wait_ge(sem, val))

        def run(e, fn=fn, emb_tok=emb_tok, inc=inc):
            ins = fn(e)
            if emb_tok is not None:
                ins._wait_ge(emb_tok[0], emb_tok[1])
            if inc is not None:
                ins.then_inc(inc[0], inc[1])
        self.q[eng].append(run)

    def op(self, eng, fn, r=(), w=(), signal=True, emb=None):
        toks = self._deps(eng, r, w)
        if emb is None:
            emb = eng != "pe"
        self.n_instr += 1
        if not signal:
            self.pend[eng][0].extend(r)
            self.pend[eng][1].extend(w)
            self._emit(eng, fn, toks, emb, None)
            return None
        if self.cnt[eng] >= SEM_ROT:
            self._rot(eng)
        self.cnt[eng] += 1
        sem, val = self.sem[eng], self.cnt[eng]
        self._emit(eng, fn, toks, emb, (sem, 1))
        tok = (sem, val)
        self.last[eng] = tok
        pr, pw = self.pend[eng]
        self._reg(tok, list(r) + pr, list(w) + pw)
        self.pend[eng] = ([], [])
        return tok

    def dma(self, eng, out, in_, r=(), w=(), key=None, **kw):
        toks = self._deps(eng, r, w)
        self.n_instr += 1
        if key is None:
            key = (tuple(w) + tuple(r))[0]
        if key not in self.slots:
            self.slots[key] = Slot(self.new_sem("d"))
        s = self.slots[key]
        s.cnt += 16
        self._emit(eng, lambda e, out=out, in_=in_, kw=kw: e.dma_start(out=out, in_=in_, **kw), toks, False, (s.sem, 16))
        tok = (s.sem, s.cnt)
        self._reg(tok, r, w)
        return tok

    def barrier(self):
        ts = [self.last[e] for e in ("pe", "act", "dve", "pool")]
        ts += [(s.sem, s.cnt) for s in self.slots.values() if s.cnt]
        for e in ENG:
            for t in ts:
                self._wait(e, t)


class Arena:
    def __init__(self, buf, base=0, limit=ARENA_BYTES):
        self.buf = buf
        self.off = base
        self.limit = limit

    def alloc(self, shape, dtype):
        esz = 4 if dtype == F32 else 2
        n = int(np.prod(shape[1:])) * esz
        off = (self.off + 63) // 64 * 64
        assert off + n <= self.limit, f"arena overflow {off}+{n}>{self.limit}"
        self.off = off + n
        ap = self.buf[0:shape[0], off:off + n].bitcast(dtype)
        if len(shape) == 3:
            ap = ap.rearrange("p (a b) -> p a b", a=shape[1])
        return ap


def v_ts(out, in0, s1, s2, op0, op1=None):
    if op1 is None:
        return lambda e: e.tensor_scalar(out=out, in0=in0, scalar1=s1, scalar2=s2, op0=op0)
    return lambda e: e.tensor_scalar(out=out, in0=in0, scalar1=s1, scalar2=s2, op0=op0, op1=op1)


def v_tt(out, a, b, op):
    return lambda e: e.tensor_tensor(out=out, in0=a, in1=b, op=op)


def v_stt(out, in0, sc, in1, op0, op1):
    return lambda e: e.scalar_tensor_tensor(out=out, in0=in0, scalar=sc, in1=in1, op0=op0, op1=op1)


def v_cp(out, in_):
    return lambda e: e.tensor_copy(out=out, in_=in_)


def v_red(out, in_, op, absv=False):
    if absv:
        return lambda e: e.tensor_reduce(out=out, in_=in_, axis=AX.X, op=op, apply_absolute_value=True)
    return lambda e: e.tensor_reduce(out=out, in_=in_, axis=AX.X, op=op)


def a_act(out, in_, func, scale=None, bias=None):
    kw = {}
    if scale is not None:
        kw["scale"] = scale
    if bias is not None:
        kw["bias"] = bias
    return lambda e: e.activation(out=out, in_=in_, func=func, **kw)


def p_mm(out, lhsT, rhs, start=True, stop=True, tp=None):
    if tp is None:
        return lambda e: e.matmul(out, lhsT=lhsT, rhs=rhs, start=start, stop=stop)
    return lambda e: e.matmul(out, lhsT=lhsT, rhs=rhs, start=start, stop=stop, tile_position=tp)


def p_tr(out, in_, ident):
    return lambda e: e.transpose(out=out, in_=in_, identity=ident)


def build(T0, T1, depth=2, stop_after=None):
    T = T0 + T1
    segs = [(0, T0), (T0, T1)]
    NKT = T // 128
    nc = bass.Bass("TRN2", target_bir_lowering=False)
    stack = ExitStack()

    def din(name, shape, dt=F32):
        return nc.dram_tensor(name, list(shape), dt, kind="ExternalInput")

    x_in = din("x_in", [T, D])
    w_in = din("w_in", [depth, D, DIN])
    w_out = din("w_out", [depth, D, D])
    w_up = din("w_up", [depth, D, DFF])
    w_down = din("w_down", [depth, DFF, D])
    norm1_g = din("norm1_g", [depth, D])
    norm2_g = din("norm2_g", [depth, D])
    conv_w = din("conv_w", [depth, 3, 512])
    q_norm_g = din("q_norm_g", [depth, 64])
    k_norm_g = din("k_norm_g", [depth, 64])
    lam_q1 = din("lam_q1", [depth, 64])
    lam_k1 = din("lam_k1", [depth, 64])
    lam_q2 = din("lam_q2", [depth, 64])
    lam_k2 = din("lam_k2", [depth, 64])
    subln_g = din("subln_g", [depth, 128])
    sgu_norm_g = din("sgu_norm_g", [depth, 128])
    sgu_w = din("sgu_w", [depth, 4, 128, 128])
    sgu_b = din("sgu_b", [depth, 4, 128])
    ropec = din("ropec", [128, T])
    ropes = din("ropes", [128, T])
    c_ident = din("c_ident", [128, 128])
    c_blk = din("c_blk", [128, 128])
    c_rmat = din("c_rmat", [128, 128])
    c_sel = din("c_sel", [16, 4])
    c_selz = din("c_selz", [64, 256])
    y_out = nc.dram_tensor("y_out", [T, D], F32, kind="ExternalOutput")

    wb = {"in": nc.dram_tensor("wb_in", [depth, D, DIN], BF16), "out": nc.dram_tensor("wb_out", [depth, D, D], BF16),
          "up": nc.dram_tensor("wb_up", [depth, D, DFF], BF16), "down": nc.dram_tensor("wb_down", [depth, DFF, D], BF16)}
    wsrc = {"in": w_in, "out": w_out, "up": w_up, "down": w_down}
    x1 = nc.dram_tensor("x1", [T, D], F32)
    qT = nc.dram_tensor("qT", [NH, 128, T], BF16)
    kT_own = [[nc.dram_tensor(f"kT_own_{h}_{si}", [128, tl], BF16) for si, (c0, tl) in enumerate(segs)] for h in range(NH)]
    kT_g = [[nc.dram_tensor(f"kT_g_{h}_{si}", [512, tl], BF16) for si, (c0, tl) in enumerate(segs)] for h in range(NH)]
    v_own = [[nc.dram_tensor(f"v_own_{h}_{si}", [128, tl], BF16) for si, (c0, tl) in enumerate(segs)] for h in range(NH)]
    v_g = [[nc.dram_tensor(f"v_g_{h}_{si}", [512, tl], BF16) for si, (c0, tl) in enumerate(segs)] for h in range(NH)]
    zb_own = nc.dram_tensor("zb_own", [4, 512], F32)
    zb_g = nc.dram_tensor("zb_g", [16, 512], F32)
    zT = nc.dram_tensor("zT", [512, T + 4], F32)
    abT = nc.dram_tensor("abT", [512, T], F32)
    mixT = nc.dram_tensor("mixT", [D, T], BF16)

    arena_t = stack.enter_context(nc.sbuf_tensor("arena", [128, ARENA_BYTES], U8))
    psA = stack.enter_context(nc.psum_tensor("psA", [128, 1024], F32))
    psB = stack.enter_context(nc.psum_tensor("psB", [128, 1024], F32))
    ps4 = [stack.enter_context(nc.psum_tensor(f"ps{i}", [128, 512], F32)) for i in range(4)]
    banks = [psA[:, 0:512], psA[:, 512:1024], psB[:, 0:512], psB[:, 512:1024]] + [p[:, :] for p in ps4]

    def BK(i):
        return ("bank", i)

    P = Sched(nc, stack)

    CA = Arena(arena_t, 0, 12 * 1024)
    ident_f = CA.alloc([128, 128], F32)
    ident = CA.alloc([128, 128], BF16)
    blk = CA.alloc([128, 128], BF16)
    ones = CA.alloc([128, 128], BF16)
    rmat = CA.alloc([128, 128], F32)
    rg_q = CA.alloc([128, 128], BF16)
    rg_k = CA.alloc([128, 128], BF16)
    selz = CA.alloc([64, 256], F32)
    sel = CA.alloc([16, 4], F32)
    gq_bc = CA.alloc([128, 64], F32)
    gk_bc = CA.alloc([128, 64], F32)
    lam_bc = CA.alloc([128, 4, 64], F32)
    lam_tmp = CA.alloc([128, 64], F32)
    cols = CA.alloc([128, 32], F32)
    convw = CA.alloc([128, 4, 3], F32)
    halo = CA.alloc([128, 4, 4], F32)
    zbg_sb = CA.alloc([16, 512], F32)
    PASS_BASE = 12 * 1024
    C_G8Q, C_G8K, C_NEGC, C_MQ, C_MK, C_S1, C_S2, C_LAM, C_NEGLAM, C_GSUB, C_GQ, C_GK, C_SUB0 = range(13)

    def col(i):
        return cols[:, i:i + 1]

    tmp_f = Arena(arena_t, PASS_BASE).alloc([128, 128], F32)
    P.dma("sp", ident_f, c_ident.ap(), w=["ident_f"])
    P.dma("sp", rmat, c_rmat.ap(), w=["rmat"])
    P.dma("sp", selz, c_selz.ap(), w=["selz"])
    P.dma("sp", sel, c_sel.ap(), w=["sel"])
    P.dma("sp", tmp_f, c_blk.ap(), w=["tmp_f"])
    P.op("dve", v_cp(ident, ident_f), r=["ident_f"], w=["ident"])
    P.op("dve", v_cp(blk, tmp_f), r=["tmp_f"], w=["blk"])
    P.op("dve", lambda e: e.memset(ones, 1.0), w=["ones"])

    wcn = {"n": 0}

    def wckey():
        wcn["n"] += 1
        return ("wc", wcn["n"] % 6)

    SLAB_ORDER = [0, 2, 1, 3, 4, 5, 6, 7, 8, 9, 10]
    conv_q = {}

    def mk_conv(dst, src_, key):
        return lambda: P.dma("pool", dst, src_, w=[key], key=wckey())

    for l in range(depth):
        qi = []
        for s in SLAB_ORDER:
            qi.append(mk_conv(wb["in"][l, :, s * 512:(s + 1) * 512], wsrc["in"][l, :, s * 512:(s + 1) * 512], ("wb", "in", l, s)))
        conv_q[("in", l)] = qi
        qo_ = []
        for n in range(4):
            qo_.append(mk_conv(wb["out"][l, :, n * 512:(n + 1) * 512], wsrc["out"][l, :, n * 512:(n + 1) * 512], ("wb", "out", l, n)))
        for s in range(16):
            qo_.append(mk_conv(wb["up"][l, :, s * 512:(s + 1) * 512], wsrc["up"][l, :, s * 512:(s + 1) * 512], ("wb", "up", l, s)))
        for n in range(4):
            for s in range(4):
                qo_.append(mk_conv(wb["down"][l, s * 2048:(s + 1) * 2048, n * 512:(n + 1) * 512],
                                   wsrc["down"][l, s * 2048:(s + 1) * 2048, n * 512:(n + 1) * 512], ("wb", "down", l, n, s)))
        conv_q[("rest", l)] = qo_

    def issue_conv(which, n=None):
        q = conv_q.get(which, [])
        k = len(q) if n is None else min(n, len(q))
        for _ in range(k):
            q.pop(0)()

    issue_conv(("in", 0))

    def layer_consts(l):
        lambda_init = 0.8 - 0.6 * math.exp(-0.3 * l)
        lck = []

        def lc(dst, src_):
            k = ("lc", len(lck))
            lck.append(k)
            P.dma("sp", dst, src_, w=[k], key="lcslot")

        lc(gq_bc, q_norm_g[l].partition_broadcast(128))
        lc(gk_bc, k_norm_g[l].partition_broadcast(128))
        for i, lm in enumerate((lam_q1, lam_k1, lam_q2, lam_k2)):
            lc(lam_bc[:, i, :], lm[l].partition_broadcast(128))
        for half in range(2):
            lc(cols[half * 64:(half + 1) * 64, C_GQ:C_GQ + 1], q_norm_g[l].rearrange("(d o) -> d o", o=1))
            lc(cols[half * 64:(half + 1) * 64, C_GK:C_GK + 1], k_norm_g[l].rearrange("(d o) -> d o", o=1))
        lc(col(C_SUB0), subln_g[l].rearrange("(d o) -> d o", o=1))
        for k in range(3):
            for c in range(4):
                lc(convw[:, c, k:k + 1], conv_w[l, k, c * 128:(c + 1) * 128].rearrange("(d o) -> d o", o=1))
        P.op("dve", lambda e: e.memset(cols[:, 30:31], 0.0), r=lck, w=["cols", "gq_bc", "gk_bc", "lam_bc", "convw"])
        CW = dict(r=["cols"], w=["cols"])
        P.op("dve", v_ts(col(C_G8Q), col(C_GQ), 1.0, None, ALU.mult), **CW)
        P.op("dve", v_ts(col(C_G8K), col(C_GK), 1.0, None, ALU.mult), **CW)
        P.op("dve", v_ts(rg_q, rmat, col(C_G8Q), None, ALU.mult), r=["cols", "rmat"], w=["rg_q"])
        P.op("dve", v_ts(rg_k, rmat, col(C_G8K), None, ALU.mult), r=["cols", "rmat"], w=["rg_k"])
        P.op("dve", v_red(col(C_MQ), gq_bc, ALU.max, True), r=["gq_bc", "cols"], w=["cols"])
        P.op("dve", v_red(col(C_MK), gk_bc, ALU.max, True), r=["gk_bc", "cols"], w=["cols"])
        P.op("dve", v_ts(col(C_NEGC), col(C_MQ), col(C_MK), -8.0, ALU.mult, ALU.mult), **CW)
        P.op("dve", v_tt(lam_tmp, lam_bc[:, 0, :], lam_bc[:, 1, :], ALU.mult), r=["lam_bc"], w=["lam_tmp"])
        P.op("dve", v_red(col(C_S1), lam_tmp, ALU.add), r=["lam_tmp", "cols"], w=["cols"])
        P.op("dve", v_tt(lam_tmp, lam_bc[:, 2, :], lam_bc[:, 3, :], ALU.mult), r=["lam_bc"], w=["lam_tmp"])
        P.op("dve", v_red(col(C_S2), lam_tmp, ALU.add), r=["lam_tmp", "cols"], w=["cols"])
        P.op("act", a_act(cols[:, C_S1:C_S2 + 1], cols[:, C_S1:C_S2 + 1], AF.Exp), **CW)
        P.op("dve", v_ts(col(C_LAM), col(C_S1), col(C_S2), lambda_init, ALU.subtract, ALU.add), **CW)
        P.op("dve", v_ts(col(C_NEGLAM), col(C_LAM), -1.0, None, ALU.mult), **CW)
        P.op("dve", v_ts(col(C_GSUB), col(C_SUB0), 1.0 - lambda_init, None, ALU.mult), **CW)

    def mk_blocks():
        blocks = []
        for si, (c0, tl) in enumerate(segs):
            for j in range(tl // 512):
                blocks.append((si, c0 + j * 512, j == 0, j == tl // 512 - 1))
        return blocks

    blocks = mk_blocks()
    NB = len(blocks)

    def gelu(acc, acc_key, out_ap, out_key, G):
        g_sq, g_xh, g_u, g_th = G
        P.op("act", a_act(g_sq, acc, AF.Square), r=[acc_key], w=["g_sq"])
        P.op("act", a_act(g_xh, acc, AF.Copy, scale=0.5), r=[acc_key], w=["g_xh"])
        P.op("dve", v_ts(g_u, g_sq, 0.044715, 1.0, ALU.mult, ALU.add), r=["g_sq"], w=["g_u"])
        P.op("dve", v_tt(g_u, g_u, g_xh, ALU.mult), r=["g_u", "g_xh"], w=["g_u"])
        P.op("act", a_act(g_th, g_u, AF.Tanh, scale=2.0 * 0.7978845608028654), r=["g_u"], w=["g_th"])
        P.op("dve", v_stt(out_ap, g_th, 1.0, g_xh, ALU.add, ALU.mult), r=["g_th", "g_xh"], w=[out_key])

    def pass1(l, xsrc):
        A = Arena(arena_t, PASS_BASE)
        g1bc = A.alloc([128, D], F32)
        xt = [A.alloc([128, D], F32) for _ in range(1)]
        ht = [A.alloc([128, D], BF16) for _ in range(1)]
        hT = [A.alloc([128, 16, 512], BF16) for _ in range(2)]
        NSL = 3
        wsl = [A.alloc([128, 16, 512], BF16) for _ in range(NSL)]
        ax = A.alloc([128, 4, 512], F32)
        zst = [A.alloc([128, 512], F32) for _ in range(2)]
        abst = [A.alloc([128, 512], F32) for _ in range(2)]
        rc = [A.alloc([128, 512], F32) for _ in range(2)]
        rs = [A.alloc([128, 512], F32) for _ in range(2)]
        sqb = [A.alloc([128, 512], BF16) for _ in range(2)]
        qb16 = [A.alloc([128, 512], BF16) for _ in range(2)]
        rstd = [A.alloc([128, 512], F32) for _ in range(2)]
        qn = [A.alloc([128, 512], F32) for _ in range(2)]
        t2 = [A.alloc([128, 512], F32) for _ in range(2)]
        qo = [A.alloc([128, 512], BF16) for _ in range(3)]
        vst = [A.alloc([128, 512], BF16) for _ in range(3)]
        ug = A.alloc([128, 4, 512], F32)
        G = [A.alloc([128, 512], F32) for _ in range(4)]
        vg = A.alloc([128, 512], F32)
        vss = A.alloc([128, 4], F32)
        vn = A.alloc([128, 4, 512], BF16)
        outc = [A.alloc([128, 512], BF16) for _ in range(2)]
        octmp = A.alloc([128, 512], F32)
        vtmp = octmp
        bsb4 = A.alloc([128, 4, 128], F32)
        gv_bc = A.alloc([128, 128], F32)
        ws_f = A.alloc([128, 4, 128], F32)
        ws_b = A.alloc([128, 4, 128], BF16)
        wsT = A.alloc([128, 4, 128], BF16)
        ssq = A.alloc([128, 8], F32)

        P.dma("sp", g1bc, norm1_g[l].partition_broadcast(128), w=["g1bc"])
        P.dma("sp", gv_bc, sgu_norm_g[l].partition_broadcast(128), w=["gv_bc"])
        P.dma("sp", ws_f, sgu_w[l].rearrange("h q p -> q h p"), w=["ws_f"])
        for hg in range(4):
            P.dma("sp", bsb4[:, hg, :], sgu_b[l, hg].partition_broadcast(128), w=["bsb4"])
        P.op("dve", v_cp(ws_b, ws_f), r=["ws_f"], w=["ws_b"])
        for hg in range(4):
            pt = banks[7].bitcast(BF16)[:, 0:128]
            P.op("pe", p_tr(pt, ws_b[:, hg, :], ident), r=["ws_b", "ident"], w=[BK(7)])
            P.op("dve", v_cp(wsT[:, hg, :], pt), r=[BK(7)], w=["wsT"])

        wv = wb["in"][l].rearrange("(kc p) c -> p kc c", p=128)
        slab_seq = [(b, s) for b in range(NB) for s in SLAB_ORDER]
        wn = {"n": 0}

        def prefetch(upto):
            while wn["n"] <= upto and wn["n"] < len(slab_seq):
                i = wn["n"]
                wn["n"] += 1
                k = i % NSL
                s = slab_seq[i][1]
                P.dma("sp", wsl[k], wv[:, :, s * 512:(s + 1) * 512], r=[("wb", "in", l, s)], w=[("wsl", k)])

        xn = {"n": 0}

        def stage_A(b):
            si, t0, _, _ = blocks[b]
            hb = b % 2
            for tt in range(4):
                i = xn["n"]
                xn["n"] += 1
                k = 0
                sc = ssq[:, (i % 8):(i % 8) + 1]
                P.dma("sp", xt[k], xsrc[t0 + tt * 128:t0 + (tt + 1) * 128, :], w=[("xt", k)])
                P.op("act", lambda e, k=k, sc=sc: e.activation(out=ht[k], in_=xt[k], func=AF.Square, accum_out=sc),
                     r=[("xt", k)], w=[("ht", k), "ssq"], emb=False)
                P.op("act", a_act(sc, sc, AF.Sqrt, scale=1.0 / D, bias=EPS), r=["ssq"], w=["ssq"])
                P.op("dve", lambda e, sc=sc: e.reciprocal(out=sc, in_=sc), r=["ssq"], w=["ssq"])
                P.op("dve", v_stt(ht[k], xt[k], sc, g1bc, ALU.mult, ALU.mult), r=[("xt", k), "ssq", "g1bc"], w=[("ht", k)])
                for g4 in range(4):
                    bi = 6 + (g4 % 2)
                    pt = banks[bi].bitcast(BF16)
                    for q in range(4):
                        fc = g4 * 4 + q
                        P.op("pe", p_tr(pt[:, q * 128:(q + 1) * 128], ht[k][:, fc * 128:(fc + 1) * 128], ident),
                             r=[("ht", k), "ident"], w=[BK(bi)], signal=(q == 3))
                    P.op("act", a_act(hT[hb][:, g4 * 4:(g4 + 1) * 4, tt * 128:(tt + 1) * 128],
                                      pt[:, 0:512].rearrange("p (a b) -> p a b", a=4), AF.Copy),
                         r=[BK(bi)], w=[("hT", hb)])
            P.dma("sp", rc[hb], ropec[:, t0:t0 + 512], w=[("rc", hb)])
            P.dma("sp", rs[hb], ropes[:, t0:t0 + 512], w=[("rs", hb)])

        accn = {"n": 0}
        cnt = {"z": 0, "ab": 0, "q": 0, "v": 0, "oc": 0, "qk": 0}

        def proj(b, k, fm, idx):
            hb = b % 2
            a = accn["n"] % 3
            accn["n"] += 1
            acc = banks[a]
            for kc in range(16):
                if fm:
                    f = p_mm(acc, wsl[k][:, kc, idx * 128:(idx + 1) * 128], hT[hb][:, kc, :], kc == 0, kc == 15)
                else:
                    f = p_mm(acc, hT[hb][:, kc, idx * 128:(idx + 1) * 128], wsl[k][:, kc, :], kc == 0, kc == 15)
                P.op("pe", f, r=[("wsl", k), ("hT", hb)], w=[BK(a)], signal=(kc == 15))
            return acc, BK(a)

        def qk_post(b, which, h, acc, ak):
            hb = b % 2
            t0 = blocks[b][1]
            w = cnt["qk"] % 2
            cnt["qk"] += 1
            rgm, rgk = (rg_q, "rg_q") if which == "q" else (rg_k, "rg_k")
            g8 = col(C_G8Q) if which == "q" else col(C_G8K)
            P.op("act", a_act(sqb[w], acc, AF.Square), r=[ak], w=[("sqb", w)])
            P.op("act", a_act(qb16[w], acc, AF.Copy), r=[ak], w=[("qb16", w)])
            return lambda: qk_post_b(b, which, h, acc, ak, w, rgm, rgk, g8)

        def qk_post_b(b, which, h, acc, ak, w, rgm, rgk, g8):
            hb = b % 2
            t0 = blocks[b][1]
            P.op("pe", p_mm(banks[3], blk, sqb[w]), r=["blk", ("sqb", w)], w=[BK(3)])
            P.op("pe", p_mm(banks[4], rgm, qb16[w]), r=[rgk, ("qb16", w)], w=[BK(4)])
            P.op("act", a_act(rstd[w], banks[3], AF.Sqrt, scale=1.0 / 64, bias=EPS), r=[BK(3)], w=[("rstd", w)])
            P.op("dve", lambda e, w=w: e.reciprocal(out=rstd[w], in_=rstd[w]), r=[("rstd", w)], w=[("rstd", w)])
            P.op("dve", v_stt(qn[w], acc, g8, rstd[w], ALU.mult, ALU.mult), r=[ak, "cols", ("rstd", w)], w=[("qn", w)])
            P.op("dve", v_tt(t2[w], banks[4], rstd[w], ALU.mult), r=[BK(4), ("rstd", w)], w=[("t2", w)])
            P.op("dve", v_tt(qn[w], qn[w], rc[hb], ALU.mult), r=[("qn", w), ("rc", hb)], w=[("qn", w)])
            P.op("dve", v_tt(t2[w], t2[w], rs[hb], ALU.mult), r=[("t2", w), ("rs", hb)], w=[("t2", w)])
            o = cnt["q"] % 3
            cnt["q"] += 1
            P.op("dve", v_tt(qo[o], qn[w], t2[w], ALU.add), r=[("qn", w), ("t2", w)], w=[("qo", o)])
            si, c0 = blocks[b][0], segs[blocks[b][0]][0]
            dst = qT[h, :, t0:t0 + 512] if which == "q" else kT_own[h][si][:, t0 - c0:t0 - c0 + 512]
            P.dma("pool", dst, qo[o], r=[("qo", o)])

        stage_A(0)
        prefetch(1)
        si_ = 0
        qk_pend = []
        for b in range(NB):
            si, t0, first, last = blocks[b]
            hb = b % 2
            segbase = 0 if si == 0 else T0 + 2
            segc0 = segs[si][0]
            for s in SLAB_ORDER:
                k = si_ % NSL
                prefetch(si_ + 2)
                si_ += 1
                issue_conv(("rest", l), 1)
                if s == 0:
                    for m in range(4):
                        acc, ak = proj(b, k, True, m)
                        P.op("act", a_act(ax[:, m, :], acc, AF.Copy), r=[ak], w=[("ax", m)])
                elif s == 2:
                    for m in range(4):
                        acc, ak = proj(b, k, True, m)
                        o = cnt["z"] % 2
                        cnt["z"] += 1
                        P.op("dve", v_tt(zst[o], acc, ax[:, m, :], ALU.mult), r=[ak, ("ax", m)], w=[("zst", o)])
                        c = segbase + 1 + (t0 - segc0)
                        P.dma("pool", zT[m * 128:(m + 1) * 128, c:c + 512], zst[o], r=[("zst", o)])
                        if first:
                            P.dma("pool", zb_own[2 * si, m * 128:(m + 1) * 128].rearrange("(d o) -> d o", o=1), zst[o][:, 0:1], r=[("zst", o)])
                        if last:
                            P.dma("pool", zb_own[2 * si + 1, m * 128:(m + 1) * 128].rearrange("(d o) -> d o", o=1), zst[o][:, 511:512], r=[("zst", o)])
                elif s == 1:
                    for m in range(4):
                        acc, ak = proj(b, k, True, m)
                        o = cnt["ab"] % 2
                        cnt["ab"] += 1
                        P.op("act", a_act(abst[o], acc, AF.Copy), r=[ak], w=[("abst", o)])
                        P.dma("pool", abT[m * 128:(m + 1) * 128, t0:t0 + 512], abst[o], r=[("abst", o)])
                elif s in (3, 4, 5, 6):
                    which = "q" if s < 5 else "k"
                    for m in range(4):
                        h = ((s - 3) % 2) * 4 + m
                        acc, ak = proj(b, k, True, m)
                        if qk_pend:
                            qk_pend.pop(0)()
                        qk_pend.append(qk_post(b, which, h, acc, ak))
                    if s == 6:
                        while qk_pend:
                            qk_pend.pop(0)()
                elif s in (7, 8):
                    for tt in range(4):
                        acc, ak = proj(b, k, False, tt)
                        o = cnt["v"] % 3
                        cnt["v"] += 1
                        P.op("act", a_act(vst[o], acc, AF.Copy), r=[ak], w=[("vst", o)])
                        kt = (t0 - segc0 + tt * 128) // 128
                        for hh in range(4):
                            h = (s - 7) * 4 + hh
                            P.dma("pool", v_own[h][si][:, kt * 128:(kt + 1) * 128], vst[o][:, hh * 128:(hh + 1) * 128], r=[("vst", o)])
                elif s == 9:
                    for m in range(4):
                        acc, ak = proj(b, k, True, m)
                        gelu(acc, ak, ug[:, m, :], ("ug", m), G)
                elif s == 10:
                    for tt in range(4):
                        acc, ak = proj(b, k, False, tt)
                        gelu(acc, ak, vg, "vg", G)
                        P.op("dve", v_tt(vtmp, vg, vg, ALU.mult), r=["vg"], w=["octmp"])
                        P.op("dve", v_red(vss, vtmp.rearrange("p (a b) -> p a b", a=4), ALU.add), r=["octmp"], w=["vss"])
                        P.op("act", a_act(vss, vss, AF.Sqrt, scale=1.0 / 128, bias=EPS), r=["vss"], w=["vss"])
                        P.op("dve", lambda e: e.reciprocal(out=vss, in_=vss), r=["vss"], w=["vss"])
                        for hg in range(4):
                            P.op("dve", v_stt(vn[:, tt, hg * 128:(hg + 1) * 128], vg[:, hg * 128:(hg + 1) * 128], vss[:, hg:hg + 1], gv_bc,
                                              ALU.mult, ALU.mult), r=["vg", "vss", "gv_bc"], w=[("vn", tt)])
                    for hg in range(4):
                        for tt in range(4):
                            P.op("pe", p_mm(banks[5][:, tt * 128:(tt + 1) * 128], vn[:, tt, hg * 128:(hg + 1) * 128], wsT[:, hg, :]),
                                 r=[("vn", tt), "wsT"], w=[BK(5)], signal=(tt == 3))
                        for tt in range(4):
                            P.op("dve", v_tt(octmp[:, tt * 128:(tt + 1) * 128], banks[5][:, tt * 128:(tt + 1) * 128], bsb4[:, hg, :], ALU.add),
                                 r=[BK(5), "bsb4"], w=["octmp"])
                        o = cnt["oc"] % 2
                        cnt["oc"] += 1
                        P.op("dve", v_tt(outc[o], octmp, ug[:, hg, :], ALU.mult), r=["octmp", ("ug", hg)], w=[("outc", o)])
                        P.dma("pool", mixT[1536 + hg * 128:1536 + (hg + 1) * 128, t0:t0 + 512], outc[o], r=[("outc", o)])
                if s == 5 and b + 1 < NB:
                    stage_A(b + 1)

    def gather():
        ccs = P.new_sem("cc")
        n = 0
        pairs = [(zb_own, zb_g)]
        for si in range(2):
            for h in range(NH):
                pairs.append((kT_own[h][si], kT_g[h][si]))
                pairs.append((v_own[h][si], v_g[h][si]))
        for src, dst in pairs:
            n += 1
            P.q["pool"].append(lambda e, src=src, dst=dst: e.collective_compute(
                "AllGather", ALU.bypass, replica_groups=GROUPS, ins=[src.ap().opt()], outs=[dst.ap().opt()]).then_inc(ccs))
        P.q["pool"].append(lambda e: e.wait_ge(ccs, n))
        P.op("pool", lambda e: e.memset(cols[:, 31:32], 0.0), w=["ccdone"])
        P.barrier()
        P.dma("sp", zbg_sb, zb_g.ap(), w=["zbg_sb"])
        for c in range(4):
            P.op("pe", p_mm(banks[7][:, 0:4], zbg_sb[:, c * 128:(c + 1) * 128], sel), r=["zbg_sb", "sel"], w=[BK(7)])
            P.op("dve", v_cp(halo[:, c, :], banks[7][:, 0:4]), r=[BK(7)], w=["halo"])

    def pass2(l):
        A = Arena(arena_t, PASS_BASE)
        SMAX = 4 * T1
        kTs = [A.alloc([128, SMAX], BF16) for _ in range(2)]
        Vs = [A.alloc([128, SMAX // 128, 128], BF16) for _ in range(2)]
        qs = [A.alloc([128, 512], BF16) for _ in range(2)]
        NE = 4
        Es = [A.alloc([128, 1024], BF16) for _ in range(NE)]
        zacc = [A.alloc([128, 1024], F32) for _ in range(2)]
        ones_f = A.alloc([128, 128], F32)
        rz = A.alloc([128, 512], F32)
        o1 = A.alloc([128, 512], F32)
        o2 = A.alloc([128, 512], F32)
        osq = A.alloc([128, 512], BF16)
        rstd = A.alloc([128, 512], F32)
        obs = [A.alloc([128, 512], BF16) for _ in range(2)]
        Sb = [psA, psB, None]
        NS = 3
        Sv = [psA[:, :], psB[:, :]]
        U1, U2 = banks[6], banks[7]
        P.op("dve", lambda e: e.memset(ones_f, 1.0), w=["ones_f"])

        groups = []
        for si, (c0, tl) in enumerate(segs):
            for h in range(NH):
                for qb in range(tl // 512):
                    groups.append((si, h, qb))
        heads = []
        for g in groups:
            if not heads or heads[-1] != (g[0], g[1]):
                heads.append((g[0], g[1]))
        head_idx = {hd: i for i, hd in enumerate(heads)}

        def load_head(hi):
            si, h = heads[hi]
            c0, tl = segs[si]
            hb = hi % 2
            nkt = tl // 128
            for r in range(4):
                P.dma("sp", kTs[hb][:, r * tl:(r + 1) * tl], kT_g[h][si][r * 128:(r + 1) * 128, :], w=[("kTs", hb)])
                P.dma("sp", Vs[hb][:, r * nkt:(r + 1) * nkt, :], v_g[h][si][r * 128:(r + 1) * 128, :],
                      w=[("Vs", hb)], key=("Vsl", hb))

        iters = []
        for gi, (si, h, qb) in enumerate(groups):
            nk = 4 * segs[si][1] // 128
            for kt in range(nk):
                iters.append((gi, kt, nk))

        sctr = {"n": 0}
        s_of = {}

        def s_halves(j):
            if j < 2:
                t = Sv[j]
                return t[:, 0:512], t[:, 512:1024], t
            return banks[4], banks[5], None

        def emit_S(i):
            gi, kt, nk = iters[i]
            si, h, qb = groups[gi]
            hb = head_idx[(si, h)] % 2
            if kt == 0:
                c0, tl = segs[si]
                P.dma("sp", qs[gi % 2], qT[h, :, c0 + qb * 512:c0 + (qb + 1) * 512], w=[("qs", gi % 2)])
            j = sctr["n"] % 2
            sctr["n"] += 1
            s_of[i] = j
            lo, hi, _ = s_halves(j)
            P.need("pe", [("kTs", hb), ("qs", gi % 2)])
            P.op("pe", p_mm(lo, kTs[hb][0:64, kt * 128:(kt + 1) * 128], qs[gi % 2][0:64, :], True, True, (0, 0)),
                 r=[("kTs", hb), ("qs", gi % 2)], w=[("S", j)], signal=False, emb=True)
            P.op("pe", p_mm(hi, kTs[hb][64:128, kt * 128:(kt + 1) * 128], qs[gi % 2][64:128, :], True, True, (64, 0)),
                 r=[("kTs", hb), ("qs", gi % 2)], w=[("S", j)], emb=True)

        def emit_rest(i):
            gi, kt, nk = iters[i]
            si, h, qb = groups[gi]
            hb = head_idx[(si, h)] % 2
            ek = i % NE
            j = s_of.pop(i)
            P.op("act", a_act(Es[ek], s_halves(j)[2], AF.Exp, scale=0.125, bias=col(C_NEGC)), r=[("S", j), "cols"], w=[("E", ek)])
            st, sp_ = (kt == 0), (kt == nk - 1)
            P.need("pe", [("Vs", hb)])
            P.op("pe", p_mm(U1, Vs[hb][:, kt, :], Es[ek][:, 0:512], st, sp_), r=[("Vs", hb), ("E", ek)], w=["U1"], signal=False, emb=True)
            P.op("pe", p_mm(U2, Vs[hb][:, kt, :], Es[ek][:, 512:1024], st, sp_), r=[("Vs", hb), ("E", ek)], w=["U2"], emb=True)
            za = zacc[gi % 2]
            zk = ("zacc", gi % 2)
            if st:
                P.op("dve", v_cp(za[:, 384:1024], Es[ek][:, 384:1024]), r=[("E", ek)], w=[(zk, "d")])
                P.op("pool", v_cp(za[:, 0:384], Es[ek][:, 0:384]), r=[("E", ek)], w=[(zk, "p")])
            else:
                P.op("dve", v_tt(za[:, 384:1024], za[:, 384:1024], Es[ek][:, 384:1024], ALU.add), r=[("E", ek), (zk, "d")], w=[(zk, "d")])
                P.op("pool", v_tt(za[:, 0:384], za[:, 0:384], Es[ek][:, 0:384], ALU.add), r=[("E", ek), (zk, "p")], w=[(zk, "p")])
            if sp_:
                epilogue(gi)

        def epilogue(gi):
            si, h, qb = groups[gi]
            c0, tl = segs[si]
            za = zacc[gi % 2]
            zk = ("zacc", gi % 2)
            AUXA, AUXB = banks[4], banks[5]
            P.op("pe", p_mm(AUXA, ones_f, za[:, 0:512]), r=["ones_f", (zk, "d"), (zk, "p")], w=[BK(4)])
            P.op("pe", p_mm(AUXB, ones_f, za[:, 512:1024]), r=["ones_f", (zk, "d"), (zk, "p")], w=[BK(5)])
            P.op("dve", lambda e: e.reciprocal(out=rz, in_=AUXA), r=[BK(4)], w=["rz"])
            P.op("dve", v_tt(o1, U1, rz, ALU.mult), r=["U1", "rz"], w=["o1"])
            P.op("dve", lambda e: e.reciprocal(out=rz, in_=AUXB), r=[BK(5)], w=["rz"])
            P.op("dve", v_stt(o2, U2, col(C_NEGLAM), rz, ALU.mult, ALU.mult), r=["U2", "rz", "cols"], w=["o2"])
            P.op("dve", v_tt(o1, o1, o2, ALU.add), r=["o1", "o2"], w=["o1"])
            P.op("dve", v_tt(osq, o1, o1, ALU.mult), r=["o1"], w=["osq"])
            P.op("pe", p_mm(AUXA, ones, osq), r=["ones", "osq"], w=[BK(4)])
            P.op("act", a_act(rstd, AUXA, AF.Ln, scale=1.0 / 128, bias=EPS), r=[BK(4)], w=["rstd2"])
            P.op("act", a_act(rstd, rstd, AF.Exp, scale=-0.5), r=["rstd2"], w=["rstd2"])
            ob = gi % 2
            P.op("dve", v_stt(obs[ob], o1, col(C_GSUB), rstd, ALU.mult, ALU.mult), r=["o1", "rstd2", "cols"], w=[("obs", ob)])
            P.dma("pool", mixT[512 + h * 128:512 + (h + 1) * 128, c0 + qb * 512:c0 + (qb + 1) * 512], obs[ob], r=[("obs", ob)])

        load_head(0)
        if len(heads) > 1:
            load_head(1)
        loaded = 2
        emit_S(0)
        for i in range(len(iters)):
            if i + 1 < len(iters):
                emit_S(i + 1)
            emit_rest(i)
            gi, kt, nk = iters[i]
            if kt == nk - 1 and (gi + 1 == len(groups) or groups[gi + 1][:2] != groups[gi][:2]):
                if loaded < len(heads):
                    load_head(loaded)
                    loaded += 1

    def pass3(l, xsrc, xdst):
        A = Arena(arena_t, PASS_BASE)
        g2bc = A.alloc([128, D], F32)
        xb = A.alloc([128, 4, D], F32)
        mix = [A.alloc([128, 16, 512], BF16) for _ in range(1)]
        h2 = A.alloc([128, D], BF16)
        h2T = mix[0]
        act = A.alloc([128, 64, 512], BF16)
        NSL = 3
        wsl = [A.alloc([128, 16, 512], BF16) for _ in range(NSL)]
        zw = [A.alloc([128, 514], F32) for _ in range(2)]
        abw = [A.alloc([128, 512], F32) for _ in range(2)]
        cacc = A.alloc([128, 512], F32)
        rl = [A.alloc([128, 512], F32) for _ in range(2)]
        ssq = A.alloc([128, 8], F32)

        P.dma("sp", g2bc, norm2_g[l].partition_broadcast(128), w=["g2bc"])
        wo = wb["out"][l].rearrange("(kc p) c -> p kc c", p=128)
        wu = wb["up"][l].rearrange("(kc p) c -> p kc c", p=128)
        wd = wb["down"][l].rearrange("(kc p) c -> p kc c", p=128)
        seq = []
        for b in range(NB):
            seq += [("out", n) for n in range(4)] + [("up", s) for s in range(16)] + [("down", n, s) for n in range(4) for s in range(4)]
        wn = {"n": 0}

        def prefetch(upto):
            while wn["n"] <= upto and wn["n"] < len(seq):
                i = wn["n"]
                wn["n"] += 1
                k = i % NSL
                it = seq[i]
                if it[0] == "out":
                    src, key = wo[:, :, it[1] * 512:(it[1] + 1) * 512], ("wb", "out", l, it[1])
                elif it[0] == "up":
                    src, key = wu[:, :, it[1] * 512:(it[1] + 1) * 512], ("wb", "up", l, it[1])
                else:
                    src, key = wd[:, it[2] * 16:(it[2] + 1) * 16, it[1] * 512:(it[1] + 1) * 512], ("wb", "down", l, it[1], it[2])
                P.dma("sp", wsl[k], src, r=[key], w=[("wsl", k)])

        wi = 0
        issue_conv(("rest", l))
        prefetch(1)
        accn = 0
        cn = 0
        for b in range(NB):
            si, t0, first, last = blocks[b]
            segbase = 0 if si == 0 else T0 + 2
            segc0 = segs[si][0]
            mx = mix[0]
            for tt in range(4):
                P.dma("sp", xb[:, tt, :], xsrc[t0 + tt * 128:t0 + (tt + 1) * 128, :], w=[("xb", tt)])
            P.dma("sp", mx[:, 4:16, :], mixT[512:2048, t0:t0 + 512].rearrange("(c p) t -> p c t", p=128), w=["mix"])
            for c in range(4):
                o = cn % 2
                cn += 1
                cz = segbase + (t0 - segc0)
                P.dma("sp", zw[o], zT[c * 128:(c + 1) * 128, cz:cz + 514], w=[("zw", o)])
                P.dma("sp", abw[o], abT[c * 128:(c + 1) * 128, t0:t0 + 512], w=[("abw", o)])
                if first:
                    P.op("dve", v_cp(zw[o][:, 0:1], halo[:, c, 2 * si:2 * si + 1]), r=["halo"], w=[("zw", o)])
                if last:
                    P.op("dve", v_cp(zw[o][:, 513:514], halo[:, c, 2 * si + 1:2 * si + 2]), r=["halo"], w=[("zw", o)])
                P.op("dve", v_ts(cacc, zw[o][:, 1:513], convw[:, c, 1:2], None, ALU.mult), r=[("zw", o), "convw"], w=["cacc"])
                P.op("dve", v_stt(cacc, zw[o][:, 0:512], convw[:, c, 0:1], cacc, ALU.mult, ALU.add), r=[("zw", o), "convw", "cacc"], w=["cacc"])
                P.op("dve", v_stt(cacc, zw[o][:, 2:514], convw[:, c, 2:3], cacc, ALU.mult, ALU.add), r=[("zw", o), "convw", "cacc"], w=["cacc"])
                P.op("dve", v_tt(mx[:, c, :], cacc, abw[o], ALU.mult), r=["cacc", ("abw", o)], w=["mix"])
            for n in range(4):
                k = wi % NSL
                prefetch(wi + 2)
                wi += 1
                for tt in range(4):
                    a = accn % 4
                    accn += 1
                    for kc in range(16):
                        P.op("pe", p_mm(banks[a], mx[:, kc, tt * 128:(tt + 1) * 128], wsl[k][:, kc, :], kc == 0, kc == 15),
                             r=["mix", ("wsl", k)], w=[BK(a)], signal=(kc == 15))
                    P.op("dve", v_tt(xb[:, tt, n * 512:(n + 1) * 512], banks[a], xb[:, tt, n * 512:(n + 1) * 512], ALU.add),
                         r=[BK(a), ("xb", tt)], w=[("xb", tt)])
            for tt in range(4):
                sc = ssq[:, tt:tt + 1]
                P.op("act", lambda e, tt=tt, sc=sc: e.activation(out=h2, in_=xb[:, tt, :], func=AF.Square, accum_out=sc),
                     r=[("xb", tt)], w=["h2", "ssq3"], emb=False)
                P.op("act", a_act(sc, sc, AF.Sqrt, scale=1.0 / D, bias=EPS), r=["ssq3"], w=["ssq3"])
                P.op("dve", lambda e, sc=sc: e.reciprocal(out=sc, in_=sc), r=["ssq3"], w=["ssq3"])
                P.op("dve", v_stt(h2, xb[:, tt, :], sc, g2bc, ALU.mult, ALU.mult), r=[("xb", tt), "ssq3", "g2bc"], w=["h2"])
                for g4 in range(4):
                    bi = 6 + (g4 % 2)
                    pt = banks[bi].bitcast(BF16)
                    for q in range(4):
                        fc = g4 * 4 + q
                        P.op("pe", p_tr(pt[:, q * 128:(q + 1) * 128], h2[:, fc * 128:(fc + 1) * 128], ident),
                             r=["h2", "ident"], w=[BK(bi)], signal=(q == 3))
                    P.op("act", a_act(h2T[:, g4 * 4:(g4 + 1) * 4, tt * 128:(tt + 1) * 128],
                                      pt[:, 0:512].rearrange("p (a b) -> p a b", a=4), AF.Copy), r=[BK(bi)], w=["mix"])
            for s in range(16):
                k = wi % NSL
                prefetch(wi + 2)
                wi += 1
                issue_conv(("in", l + 1), 1)
                for m in range(4):
                    a = accn % 4
                    accn += 1
                    for kc in range(16):
                        P.op("pe", p_mm(banks[a], wsl[k][:, kc, m * 128:(m + 1) * 128], h2T[:, kc, :], kc == 0, kc == 15),
                             r=["mix", ("wsl", k)], w=[BK(a)], signal=(kc == 15))
                    ro = accn % 2
                    P.op("act", a_act(rl[ro], banks[a], AF.Relu), r=[BK(a)], w=[("rl", ro)])
                    P.op("pool", v_tt(act[:, s * 4 + m, :], rl[ro], rl[ro], ALU.mult), r=[("rl", ro)], w=["act"])
            for n in range(4):
                base = 0 if n % 2 == 0 else 4
                for s in range(4):
                    k = wi % NSL
                    prefetch(wi + 2)
                    wi += 1
                    for fc in range(16):
                        for tt in range(4):
                            lastmm = (s == 3 and fc == 15)
                            P.op("pe", p_mm(banks[base + tt], act[:, s * 16 + fc, tt * 128:(tt + 1) * 128], wsl[k][:, fc, :],
                                            s == 0 and fc == 0, lastmm),
                                 r=["act", ("wsl", k)], w=[BK(base + tt)], signal=(lastmm or (fc == 15 and tt == 3)))
                for tt in range(4):
                    P.op("dve", v_tt(xb[:, tt, n * 512:(n + 1) * 512], banks[base + tt], xb[:, tt, n * 512:(n + 1) * 512], ALU.add),
                         r=[BK(base + tt), ("xb", tt)], w=[("xb", tt)])
            for tt in range(4):
                P.dma("pool", xdst[t0 + tt * 128:t0 + (tt + 1) * 128, :], xb[:, tt, :], r=[("xb", tt)])

    for l in range(depth):
        xsrc = x_in if l == 0 else x1
        xdst = y_out if l == depth - 1 else x1
        layer_consts(l)
        issue_conv(("in", l))
        pass1(l, xsrc)
        P.barrier()
        if stop_after == ("p1", l):
            break
        gather()
        if stop_after == ("g", l):
            break
        pass2(l)
        P.barrier()
        if stop_after == ("p2", l):
            break
        pass3(l, xsrc, xdst)
        P.barrier()

    with nc.Block() as block:
        @block.tensor
        def _(e):
            for f in P.q["pe"]:
                f(e)

        @block.scalar
        def _(e):
            for f in P.q["act"]:
                f(e)

        @block.vector
        def _(e):
            for f in P.q["dve"]:
                f(e)

        @block.gpsimd
        def _(e):
            for f in P.q["pool"]:
                f(e)

        @block.sync
        def _(e):
            for f in P.q["sp"]:
                f(e)
    stack.close()
    print("instr counts", {k: len(v) for k, v in P.q.items()}, "nsem", P.nsem, flush=True)
    return nc


def host_consts(T0, T1, core):
    r = core % 4
    pos = np.concatenate([r * T0 + np.arange(T0), r * T1 + np.arange(T1)]).astype(np.float32)
    inv = (ROPE_THETA ** (-np.arange(0, 16, 2, dtype=np.float32) / 16)).astype(np.float32)
    ang = pos[:, None] * inv[None, :]
    cs, sn = np.cos(ang).astype(np.float32), np.sin(ang).astype(np.float32)
    T = T0 + T1
    C = np.ones((128, T), np.float32)
    S = np.zeros((128, T), np.float32)
    for gb in (0, 64):
        for d in range(16):
            C[gb + d] = cs[:, d % 8]
            S[gb + d] = sn[:, d % 8]
    ident = np.eye(128, dtype=np.float32)
    blk = np.zeros((128, 128), np.float32)
    blk[:64, :64] = 1
    blk[64:, 64:] = 1
    rmat = np.zeros((128, 128), np.float32)
    for gb in (0, 64):
        for m in range(8):
            rmat[gb + m + 8, gb + m] = -1.0
        for m in range(8, 16):
            rmat[gb + m - 8, gb + m] = 1.0
    sel = np.zeros((16, 4), np.float32)
    for s in range(2):
        if r > 0:
            sel[(r - 1) * 4 + 2 * s + 1, 2 * s] = 1.0
        if r < 3:
            sel[(r + 1) * 4 + 2 * s, 2 * s + 1] = 1.0
    selz = np.zeros((64, 256), np.float32)
    selz[0, 0:128] = 1.0
    selz[32, 128:256] = 1.0
    return {"ropec": C, "ropes": S, "c_ident": ident, "c_blk": blk, "c_rmat": rmat, "c_sel": sel, "c_selz": selz}


_CACHE = {}


def run(inputs, T0, T1, depth=2, stop_after=None, trace=False):
    key = (T0, T1, depth, stop_after)
    if key not in _CACHE:
        _CACHE[key] = build(T0, T1, depth, stop_after)
    nc = _CACHE[key]
    xp = np.asarray(inputs["x_prompt"], np.float32)
    xs = np.asarray(inputs["x_sample"], np.float32)
    shared = {k: np.ascontiguousarray(np.asarray(inputs[k], np.float32)) for k in (
        "w_in", "w_out", "w_up", "w_down", "norm1_g", "norm2_g", "conv_w", "q_norm_g", "k_norm_g",
        "lam_q1", "lam_k1", "lam_q2", "lam_k2", "subln_g", "sgu_norm_g", "sgu_w", "sgu_b")}
    in_maps = []
    for c in range(NCORES):
        g, r = c // 4, c % 4
        m = dict(shared)
        m["x_in"] = np.ascontiguousarray(np.concatenate([xp[g, r * T0:(r + 1) * T0], xs[g, r * T1:(r + 1) * T1]], 0))
        m.update(host_consts(T0, T1, c))
        in_maps.append(m)
    res = run_bass_kernel_spmd(nc, in_maps, core_ids=list(range(NCORES)), **({"trace": True} if trace else {}))
    yp = np.zeros((2, 4 * T0, D), np.float32)
    ys = np.zeros((2, 4 * T1, D), np.float32)
    for c in range(NCORES):
        g, r = c // 4, c % 4
        y = np.asarray(res.results[c]["y_out"], np.float32)
        yp[g, r * T0:(r + 1) * T0] = y[:T0]
        ys[g, r * T1:(r + 1) * T1] = y[T0:]
    return (yp, ys), res


def kernel(**inputs):
    (yp, ys), _ = run(inputs, 1024, 4096, 2)
    return (yp, ys)
```

```python
import math
from contextlib import ExitStack

import numpy as np
import concourse.bass as bass
import concourse.mybir as mybir
from concourse.bass_utils import run_bass_kernel_spmd

F32 = mybir.dt.float32
BF16 = mybir.dt.bfloat16
U8 = mybir.dt.uint8
AF = mybir.ActivationFunctionType
ALU = mybir.AluOpType
AX = mybir.AxisListType

D = 2048
DIN = 5632
DFF = 8192
NH = 8
EPS = 1e-6
ROPE_THETA = 500000.0
NCORES = 8
GROUPS = [[0, 1, 2, 3], [4, 5, 6, 7]]
ENG = ("pe", "act", "dve", "pool", "sp")
ARENA_BYTES = 206 * 1024
SEM_ROT = 30000
EMBED_WAIT = False
SKIP_SAME_ENGINE = False


class Slot:
    def __init__(self, sem):
        self.sem = sem
        self.cnt = 0


class Sched:
    def __init__(self, nc, stack):
        self.nc = nc
        self.stack = stack
        self.q = {e: [] for e in ENG}
        self.sem = {}
        self.cnt = {}
        self.nsem = 0
        self.waited = {e: {} for e in ENG}
        self.last = {e: None for e in ENG}
        self.lw = {}
        self.lr = {}
        self.pend = {e: ([], []) for e in ENG}
        self.slots = {}
        self.owner = {}
        self.n_instr = 0
        for e in ENG:
            self._rot(e)

    def new_sem(self, name):
        self.nsem += 1
        return self.stack.enter_context(self.nc.semaphore(f"{name}{self.nsem}"))

    def _rot(self, e):
        self.sem[e] = self.new_sem("s" + e)
        self.owner[id(self.sem[e])] = e
        self.cnt[e] = 0

    def _need(self, eng, tok, out):
        if tok is None:
            return
        sem, val = tok
        w = self.waited[eng]
        if w.get(id(sem), 0) >= val:
            return
        w[id(sem)] = val
        out[id(sem)] = (sem, val)

    def _wait(self, eng, tok):
        out = {}
        self._need(eng, tok, out)
        for sem, val in out.values():
            self.q[eng].append(lambda e, sem=sem, val=val: e.wait_ge(sem, val))

    def _deps(self, eng, r, w):
        out = {}
        own = self.owner
        for k in r:
            self._need(eng, self.lw.get(k), out)
        for k in w:
            t = self.lw.get(k)
            if t is not None and (not SKIP_SAME_ENGINE or own.get(id(t[0])) != eng):
                self._need(eng, t, out)
            for t in self.lr.get(k, ()):
                if not SKIP_SAME_ENGINE or own.get(id(t[0])) != eng:
                    self._need(eng, t, out)
        return list(out.values())

    def war_tokens(self, keys):
        ts = []
        for k in keys:
            if self.lw.get(k) is not None:
                ts.append(self.lw[k])
            ts.extend(self.lr.get(k, ()))
        return ts

    def wait_toks(self, eng, toks):
        for t in toks:
            self._wait(eng, t)

    def assume(self, eng, toks):
        w = self.waited[eng]
        for sem, val in toks:
            if w.get(id(sem), 0) < val:
                w[id(sem)] = val

    def need(self, eng, keys):
        for sem, val in self._deps(eng, keys, ()):
            self.q[eng].append(lambda e, sem=sem, val=val: e.wait_ge(sem, val))

    def _reg(self, tok, r, w):
        for k in r:
            self.lr.setdefault(k, []).append(tok)
        for k in w:
            self.lw[k] = tok
            self.lr[k] = []

    def _emit(self, eng, fn, toks, emb, inc):
        emb_tok = toks.pop() if (emb and EMBED_WAIT and toks) else None
        for sem, val in toks:
            self.q[eng].append(lambda e, sem=sem, val=val: e.wait_ge(sem, val))

        def run(e, fn=fn, emb_tok=emb_tok, inc=inc):
            ins = fn(e)
            if emb_tok is not None:
                ins._wait_ge(emb_tok[0], emb_tok[1])
            if inc is not None:
                ins.then_inc(inc[0], inc[1])
        self.q[eng].append(run)

    def op(self, eng, fn, r=(), w=(), signal=True, emb=None):
        toks = self._deps(eng, r, w)
        if emb is None:
            emb = eng != "pe"
        self.n_instr += 1
        if not signal:
            self.pend[eng][0].extend(r)
            self.pend[eng][1].extend(w)
            self._emit(eng, fn, toks, emb, None)
            return None
        if self.cnt[eng] >= SEM_ROT:
            self._rot(eng)
        self.cnt[eng] += 1
        sem, val = self.sem[eng], self.cnt[eng]
        self._emit(eng, fn, toks, emb, (sem, 1))
        tok = (sem, val)
        self.last[eng] = tok
        pr, pw = self.pend[eng]
        self._reg(tok, list(r) + pr, list(w) + pw)
        self.pend[eng] = ([], [])
        return tok

    def dma(self, eng, out, in_, r=(), w=(), key=None, **kw):
        toks = self._deps(eng, r, w)
        self.n_instr += 1
        if key is None:
            key = (tuple(w) + tuple(r))[0]
        if key not in self.slots:
            self.slots[key] = Slot(self.new_sem("d"))
        s = self.slots[key]
        s.cnt += 16
        self._emit(eng, lambda e, out=out, in_=in_, kw=kw: e.dma_start(out=out, in_=in_, **kw), toks, False, (s.sem, 16))
        tok = (s.sem, s.cnt)
        self._reg(tok, r, w)
        return tok

    def barrier(self):
        ts = [self.last[e] for e in ("pe", "act", "dve", "pool")]
        ts += [(s.sem, s.cnt) for s in self.slots.values() if s.cnt]
        for e in ENG:
            for t in ts:
                self._wait(e, t)


class Arena:
    def __init__(self, buf, base=0, limit=ARENA_BYTES):
        self.buf = buf
        self.off = base
        self.limit = limit

    def alloc(self, shape, dtype):
        esz = 4 if dtype == F32 else 2
        n = int(np.prod(shape[1:])) * esz
        off = (self.off + 63) // 64 * 64
        assert off + n <= self.limit, f"arena overflow {off}+{n}>{self.limit}"
        self.off = off + n
        ap = self.buf[0:shape[0], off:off + n].bitcast(dtype)
        if len(shape) == 3:
            ap = ap.rearrange("p (a b) -> p a b", a=shape[1])
        return ap


def v_ts(out, in0, s1, s2, op0, op1=None):
    if op1 is None:
        return lambda e: e.tensor_scalar(out=out, in0=in0, scalar1=s1, scalar2=s2, op0=op0)
    return lambda e: e.tensor_scalar(out=out, in0=in0, scalar1=s1, scalar2=s2, op0=op0, op1=op1)


def v_tt(out, a, b, op):
    return lambda e: e.tensor_tensor(out=out, in0=a, in1=b, op=op)


def v_stt(out, in0, sc, in1, op0, op1):
    return lambda e: e.scalar_tensor_tensor(out=out, in0=in0, scalar=sc, in1=in1, op0=op0, op1=op1)


def v_cp(out, in_):
    return lambda e: e.tensor_copy(out=out, in_=in_)


def v_red(out, in_, op, absv=False):
    if absv:
        return lambda e: e.tensor_reduce(out=out, in_=in_, axis=AX.X, op=op, apply_absolute_value=True)
    return lambda e: e.tensor_reduce(out=out, in_=in_, axis=AX.X, op=op)


def a_act(out, in_, func, scale=None, bias=None):
    kw = {}
    if scale is not None:
        kw["scale"] = scale
    if bias is not None:
        kw["bias"] = bias
    return lambda e: e.activation(out=out, in_=in_, func=func, **kw)


def p_mm(out, lhsT, rhs, start=True, stop=True, tp=None):
    if tp is None:
        return lambda e: e.matmul(out, lhsT=lhsT, rhs=rhs, start=start, stop=stop)
    return lambda e: e.matmul(out, lhsT=lhsT, rhs=rhs, start=start, stop=stop, tile_position=tp)


def p_tr(out, in_, ident):
    return lambda e: e.transpose(out=out, in_=in_, identity=ident)


def build(T0, T1, depth=2, stop_after=None):
    T = T0 + T1
    segs = [(0, T0), (T0, T1)]
    NKT = T // 128
    nc = bass.Bass("TRN2", target_bir_lowering=False)
    stack = ExitStack()

    def din(name, shape, dt=F32):
        return nc.dram_tensor(name, list(shape), dt, kind="ExternalInput")

    x_in = din("x_in", [T, D])
    w_in = din("w_in", [depth, D, DIN])
    w_out = din("w_out", [depth, D, D])
    w_up = din("w_up", [depth, D, DFF])
    w_down = din("w_down", [depth, DFF, D])
    norm1_g = din("norm1_g", [depth, D])
    norm2_g = din("norm2_g", [depth, D])
    conv_w = din("conv_w", [depth, 3, 512])
    q_norm_g = din("q_norm_g", [depth, 64])
    k_norm_g = din("k_norm_g", [depth, 64])
    lam_q1 = din("lam_q1", [depth, 64])
    lam_k1 = din("lam_k1", [depth, 64])
    lam_q2 = din("lam_q2", [depth, 64])
    lam_k2 = din("lam_k2", [depth, 64])
    subln_g = din("subln_g", [depth, 128])
    sgu_norm_g = din("sgu_norm_g", [depth, 128])
    sgu_w = din("sgu_w", [depth, 4, 128, 128])
    sgu_b = din("sgu_b", [depth, 4, 128])
    ropec = din("ropec", [128, T])
    ropes = din("ropes", [128, T])
    c_ident = din("c_ident", [128, 128])
    c_blk = din("c_blk", [128, 128])
    c_rmat = din("c_rmat", [128, 128])
    c_sel = din("c_sel", [16, 4])
    c_selz = din("c_selz", [64, 256])
    y_out = nc.dram_tensor("y_out", [T, D], F32, kind="ExternalOutput")

    wb = {"in": nc.dram_tensor("wb_in", [depth, D, DIN], BF16), "out": nc.dram_tensor("wb_out", [depth, D, D], BF16),
          "up": nc.dram_tensor("wb_up", [depth, D, DFF], BF16), "down": nc.dram_tensor("wb_down", [depth, DFF, D], BF16)}
    wsrc = {"in": w_in, "out": w_out, "up": w_up, "down": w_down}
    x1 = nc.dram_tensor("x1", [T, D], F32)
    qT = nc.dram_tensor("qT", [NH, 128, T], BF16)
    kT_own = [[nc.dram_tensor(f"kT_own_{h}_{si}", [128, tl], BF16) for si, (c0, tl) in enumerate(segs)] for h in range(NH)]
    kT_g = [[nc.dram_tensor(f"kT_g_{h}_{si}", [512, tl], BF16) for si, (c0, tl) in enumerate(segs)] for h in range(NH)]
    v_own = [[nc.dram_tensor(f"v_own_{h}_{si}", [128, tl], BF16) for si, (c0, tl) in enumerate(segs)] for h in range(NH)]
    v_g = [[nc.dram_tensor(f"v_g_{h}_{si}", [512, tl], BF16) for si, (c0, tl) in enumerate(segs)] for h in range(NH)]
    zb_own = nc.dram_tensor("zb_own", [4, 512], F32)
    zb_g = nc.dram_tensor("zb_g", [16, 512], F32)
    zT = nc.dram_tensor("zT", [512, T + 4], F32)
    abT = nc.dram_tensor("abT", [512, T], F32)
    mixT = nc.dram_tensor("mixT", [D, T], BF16)

    arena_t = stack.enter_context(nc.sbuf_tensor("arena", [128, ARENA_BYTES], U8))
    psA = stack.enter_context(nc.psum_tensor("psA", [128, 1024], F32))
    psB = stack.enter_context(nc.psum_tensor("psB", [128, 1024], F32))
    ps4 = [stack.enter_context(nc.psum_tensor(f"ps{i}", [128, 512], F32)) for i in range(4)]
    banks = [psA[:, 0:512], psA[:, 512:1024], psB[:, 0:512], psB[:, 512:1024]] + [p[:, :] for p in ps4]

    def BK(i):
        return ("bank", i)

    P = Sched(nc, stack)

    CA = Arena(arena_t, 0, 12 * 1024)
    ident_f = CA.alloc([128, 128], F32)
    ident = CA.alloc([128, 128], BF16)
    blk = CA.alloc([128, 128], BF16)
    ones = CA.alloc([128, 128], BF16)
    rmat = CA.alloc([128, 128], F32)
    rg_q = CA.alloc([128, 128], BF16)
    rg_k = CA.alloc([128, 128], BF16)
    selz = CA.alloc([64, 256], F32)
    sel = CA.alloc([16, 4], F32)
    gq_bc = CA.alloc([128, 64], F32)
    gk_bc = CA.alloc([128, 64], F32)
    lam_bc = CA.alloc([128, 4, 64], F32)
    lam_tmp = CA.alloc([128, 64], F32)
    cols = CA.alloc([128, 32], F32)
    convw = CA.alloc([128, 4, 3], F32)
    halo = CA.alloc([128, 4, 4], F32)
    zbg_sb = CA.alloc([16, 512], F32)
    PASS_BASE = 12 * 1024
    C_G8Q, C_G8K, C_NEGC, C_MQ, C_MK, C_S1, C_S2, C_LAM, C_NEGLAM, C_GSUB, C_GQ, C_GK, C_SUB0 = range(13)

    def col(i):
        return cols[:, i:i + 1]

    tmp_f = Arena(arena_t, PASS_BASE).alloc([128, 128], F32)
    P.dma("sp", ident_f, c_ident.ap(), w=["ident_f"])
    P.dma("sp", rmat, c_rmat.ap(), w=["rmat"])
    P.dma("sp", selz, c_selz.ap(), w=["selz"])
    P.dma("sp", sel, c_sel.ap(), w=["sel"])
    P.dma("sp", tmp_f, c_blk.ap(), w=["tmp_f"])
    P.op("dve", v_cp(ident, ident_f), r=["ident_f"], w=["ident"])
    P.op("dve", v_cp(blk, tmp_f), r=["tmp_f"], w=["blk"])
    P.op("dve", lambda e: e.memset(ones, 1.0), w=["ones"])

    wcn = {"n": 0}

    def wckey():
        wcn["n"] += 1
        return ("wc", wcn["n"] % 6)

    SLAB_ORDER = [0, 2, 1, 3, 4, 5, 6, 7, 8, 9, 10]
    conv_q = {}

    def mk_conv(dst, src_, key):
        return lambda: P.dma("pool", dst, src_, w=[key], key=wckey())

    for l in range(depth):
        qi = []
        for s in SLAB_ORDER:
            qi.append(mk_conv(wb["in"][l, :, s * 512:(s + 1) * 512], wsrc["in"][l, :, s * 512:(s + 1) * 512], ("wb", "in", l, s)))
        conv_q[("in", l)] = qi
        qo_ = []
        for n in range(4):
            qo_.append(mk_conv(wb["out"][l, :, n * 512:(n + 1) * 512], wsrc["out"][l, :, n * 512:(n + 1) * 512], ("wb", "out", l, n)))
        for s in range(16):
            qo_.append(mk_conv(wb["up"][l, :, s * 512:(s + 1) * 512], wsrc["up"][l, :, s * 512:(s + 1) * 512], ("wb", "up", l, s)))
        for n in range(4):
            for s in range(4):
                qo_.append(mk_conv(wb["down"][l, s * 2048:(s + 1) * 2048, n * 512:(n + 1) * 512],
                                   wsrc["down"][l, s * 2048:(s + 1) * 2048, n * 512:(n + 1) * 512], ("wb", "down", l, n, s)))
        conv_q[("rest", l)] = qo_

    def issue_conv(which, n=None):
        q = conv_q.get(which, [])
        k = len(q) if n is None else min(n, len(q))
        for _ in range(k):
            q.pop(0)()

    issue_conv(("in", 0))

    def layer_consts(l):
        lambda_init = 0.8 - 0.6 * math.exp(-0.3 * l)
        lck = []

        def lc(dst, src_):
            k = ("lc", len(lck))
            lck.append(k)
            P.dma("sp", dst, src_, w=[k], key="lcslot")

        lc(gq_bc, q_norm_g[l].partition_broadcast(128))
        lc(gk_bc, k_norm_g[l].partition_broadcast(128))
        for i, lm in enumerate((lam_q1, lam_k1, lam_q2, lam_k2)):
            lc(lam_bc[:, i, :], lm[l].partition_broadcast(128))
        for half in range(2):
            lc(cols[half * 64:(half + 1) * 64, C_GQ:C_GQ + 1], q_norm_g[l].rearrange("(d o) -> d o", o=1))
            lc(cols[half * 64:(half + 1) * 64, C_GK:C_GK + 1], k_norm_g[l].rearrange("(d o) -> d o", o=1))
        lc(col(C_SUB0), subln_g[l].rearrange("(d o) -> d o", o=1))
        for k in range(3):
            for c in range(4):
                lc(convw[:, c, k:k + 1], conv_w[l, k, c * 128:(c + 1) * 128].rearrange("(d o) -> d o", o=1))
        P.op("dve", lambda e: e.memset(cols[:, 30:31], 0.0), r=lck, w=["cols", "gq_bc", "gk_bc", "lam_bc", "convw"])
        CW = dict(r=["cols"], w=["cols"])
        P.op("dve", v_ts(col(C_G8Q), col(C_GQ), 1.0, None, ALU.mult), **CW)
        P.op("dve", v_ts(col(C_G8K), col(C_GK), 1.0, None, ALU.mult), **CW)
        P.op("dve", v_ts(rg_q, rmat, col(C_G8Q), None, ALU.mult), r=["cols", "rmat"], w=["rg_q"])
        P.op("dve", v_ts(rg_k, rmat, col(C_G8K), None, ALU.mult), r=["cols", "rmat"], w=["rg_k"])
        P.op("dve", v_red(col(C_MQ), gq_bc, ALU.max, True), r=["gq_bc", "cols"], w=["cols"])
        P.op("dve", v_red(col(C_MK), gk_bc, ALU.max, True), r=["gk_bc", "cols"], w=["cols"])
        P.op("dve", v_ts(col(C_NEGC), col(C_MQ), col(C_MK), -8.0, ALU.mult, ALU.mult), **CW)
        P.op("dve", v_tt(lam_tmp, lam_bc[:, 0, :], lam_bc[:, 1, :], ALU.mult), r=["lam_bc"], w=["lam_tmp"])
        P.op("dve", v_red(col(C_S1), lam_tmp, ALU.add), r=["lam_tmp", "cols"], w=["cols"])
        P.op("dve", v_tt(lam_tmp, lam_bc[:, 2, :], lam_bc[:, 3, :], ALU.mult), r=["lam_bc"], w=["lam_tmp"])
        P.op("dve", v_red(col(C_S2), lam_tmp, ALU.add), r=["lam_tmp", "cols"], w=["cols"])
        P.op("act", a_act(cols[:, C_S1:C_S2 + 1], cols[:, C_S1:C_S2 + 1], AF.Exp), **CW)
        P.op("dve", v_ts(col(C_LAM), col(C_S1), col(C_S2), lambda_init, ALU.subtract, ALU.add), **CW)
        P.op("dve", v_ts(col(C_NEGLAM), col(C_LAM), -1.0, None, ALU.mult), **CW)
        P.op("dve", v_ts(col(C_GSUB), col(C_SUB0), 1.0 - lambda_init, None, ALU.mult), **CW)

    def mk_blocks():
        blocks = []
        for si, (c0, tl) in enumerate(segs):
            for j in range(tl // 512):
                blocks.append((si, c0 + j * 512, j == 0, j == tl // 512 - 1))
        return blocks

    blocks = mk_blocks()
    NB = len(blocks)

    def gelu(acc, acc_key, out_ap, out_key, G):
        g_sq, g_xh, g_u, g_th = G
        P.op("act", a_act(g_sq, acc, AF.Square), r=[acc_key], w=["g_sq"])
        P.op("act", a_act(g_xh, acc, AF.Copy, scale=0.5), r=[acc_key], w=["g_xh"])
        P.op("dve", v_ts(g_u, g_sq, 0.044715, 1.0, ALU.mult, ALU.add), r=["g_sq"], w=["g_u"])
        P.op("dve", v_tt(g_u, g_u, g_xh, ALU.mult), r=["g_u", "g_xh"], w=["g_u"])
        P.op("act", a_act(g_th, g_u, AF.Tanh, scale=2.0 * 0.7978845608028654), r=["g_u"], w=["g_th"])
        P.op("dve", v_stt(out_ap, g_th, 1.0, g_xh, ALU.add, ALU.mult), r=["g_th", "g_xh"], w=[out_key])

    def pass1(l, xsrc):
        A = Arena(arena_t, PASS_BASE)
        g1bc = A.alloc([128, D], F32)
        xt = [A.alloc([128, D], F32) for _ in range(1)]
        ht = [A.alloc([128, D], BF16) for _ in range(1)]
        hT = [A.alloc([128, 16, 512], BF16) for _ in range(2)]
        NSL = 3
        wsl = [A.alloc([128, 16, 512], BF16) for _ in range(NSL)]
        ax = A.alloc([128, 4, 512], F32)
        zst = [A.alloc([128, 512], F32) for _ in range(2)]
        abst = [A.alloc([128, 512], F32) for _ in range(2)]
        rc = [A.alloc([128, 512], F32) for _ in range(2)]
        rs = [A.alloc([128, 512], F32) for _ in range(2)]
        sqb = [A.alloc([128, 512], BF16) for _ in range(2)]
        qb16 = [A.alloc([128, 512], BF16) for _ in range(2)]
        rstd = [A.alloc([128, 512], F32) for _ in range(2)]
        qn = [A.alloc([128, 512], F32) for _ in range(2)]
        t2 = [A.alloc([128, 512], F32) for _ in range(2)]
        qo = [A.alloc([128, 512], BF16) for _ in range(3)]
        vst = [A.alloc([128, 512], BF16) for _ in range(3)]
        ug = A.alloc([128, 4, 512], F32)
        G = [A.alloc([128, 512], F32) for _ in range(4)]
        vg = A.alloc([128, 512], F32)
        vss = A.alloc([128, 4], F32)
        vn = A.alloc([128, 4, 512], BF16)
        outc = [A.alloc([128, 512], BF16) for _ in range(2)]
        octmp = A.alloc([128, 512], F32)
        vtmp = octmp
        bsb4 = A.alloc([128, 4, 128], F32)
        gv_bc = A.alloc([128, 128], F32)
        ws_f = A.alloc([128, 4, 128], F32)
        ws_b = A.alloc([128, 4, 128], BF16)
        wsT = A.alloc([128, 4, 128], BF16)
        ssq = A.alloc([128, 8], F32)

        P.dma("sp", g1bc, norm1_g[l].partition_broadcast(128), w=["g1bc"])
        P.dma("sp", gv_bc, sgu_norm_g[l].partition_broadcast(128), w=["gv_bc"])
        P.dma("sp", ws_f, sgu_w[l].rearrange("h q p -> q h p"), w=["ws_f"])
        for hg in range(4):
            P.dma("sp", bsb4[:, hg, :], sgu_b[l, hg].partition_broadcast(128), w=["bsb4"])
        P.op("dve", v_cp(ws_b, ws_f), r=["ws_f"], w=["ws_b"])
        for hg in range(4):
            pt = banks[7].bitcast(BF16)[:, 0:128]
            P.op("pe", p_tr(pt, ws_b[:, hg, :], ident), r=["ws_b", "ident"], w=[BK(7)])
            P.op("dve", v_cp(wsT[:, hg, :], pt), r=[BK(7)], w=["wsT"])

        wv = wb["in"][l].rearrange("(kc p) c -> p kc c", p=128)
        slab_seq = [(b, s) for b in range(NB) for s in SLAB_ORDER]
        wn = {"n": 0}

        def prefetch(upto):
            while wn["n"] <= upto and wn["n"] < len(slab_seq):
                i = wn["n"]
                wn["n"] += 1
                k = i % NSL
                s = slab_seq[i][1]
                P.dma("sp", wsl[k], wv[:, :, s * 512:(s + 1) * 512], r=[("wb", "in", l, s)], w=[("wsl", k)])

        xn = {"n": 0}

        def stage_A(b):
            si, t0, _, _ = blocks[b]
            hb = b % 2
            for tt in range(4):
                i = xn["n"]
                xn["n"] += 1
                k = 0
                sc = ssq[:, (i % 8):(i % 8) + 1]
                P.dma("sp", xt[k], xsrc[t0 + tt * 128:t0 + (tt + 1) * 128, :], w=[("xt", k)])
                P.op("act", lambda e, k=k, sc=sc: e.activation(out=ht[k], in_=xt[k], func=AF.Square, accum_out=sc),
                     r=[("xt", k)], w=[("ht", k), "ssq"], emb=False)
                P.op("act", a_act(sc, sc, AF.Sqrt, scale=1.0 / D, bias=EPS), r=["ssq"], w=["ssq"])
                P.op("dve", lambda e, sc=sc: e.reciprocal(out=sc, in_=sc), r=["ssq"], w=["ssq"])
                P.op("dve", v_stt(ht[k], xt[k], sc, g1bc, ALU.mult, ALU.mult), r=[("xt", k), "ssq", "g1bc"], w=[("ht", k)])
                for g4 in range(4):
                    bi = 6 + (g4 % 2)
                    pt = banks[bi].bitcast(BF16)
                    for q in range(4):
                        fc = g4 * 4 + q
                        P.op("pe", p_tr(pt[:, q * 128:(q + 1) * 128], ht[k][:, fc * 128:(fc + 1) * 128], ident),
                             r=[("ht", k), "ident"], w=[BK(bi)], signal=(q == 3))
                    P.op("act", a_act(hT[hb][:, g4 * 4:(g4 + 1) * 4, tt * 128:(tt + 1) * 128],
                                      pt[:, 0:512].rearrange("p (a b) -> p a b", a=4), AF.Copy),
                         r=[BK(bi)], w=[("hT", hb)])
            P.dma("sp", rc[hb], ropec[:, t0:t0 + 512], w=[("rc", hb)])
            P.dma("sp", rs[hb], ropes[:, t0:t0 + 512], w=[("rs", hb)])

        accn = {"n": 0}
        cnt = {"z": 0, "ab": 0, "q": 0, "v": 0, "oc": 0, "qk": 0}

        def proj(b, k, fm, idx):
            hb = b % 2
            a = accn["n"] % 3
            accn["n"] += 1
            acc = banks[a]
            for kc in range(16):
                if fm:
                    f = p_mm(acc, wsl[k][:, kc, idx * 128:(idx + 1) * 128], hT[hb][:, kc, :], kc == 0, kc == 15)
                else:
                    f = p_mm(acc, hT[hb][:, kc, idx * 128:(idx + 1) * 128], wsl[k][:, kc, :], kc == 0, kc == 15)
                P.op("pe", f, r=[("wsl", k), ("hT", hb)], w=[BK(a)], signal=(kc == 15))
            return acc, BK(a)

        def qk_post(b, which, h, acc, ak):
            hb = b % 2
            t0 = blocks[b][1]
            w = cnt["qk"] % 2
            cnt["qk"] += 1
            rgm, rgk = (rg_q, "rg_q") if which == "q" else (rg_k, "rg_k")
            g8 = col(C_G8Q) if which == "q" else col(C_G8K)
            P.op("act", a_act(sqb[w], acc, AF.Square), r=[ak], w=[("sqb", w)])
            P.op("act", a_act(qb16[w], acc, AF.Copy), r=[ak], w=[("qb16", w)])
            return lambda: qk_post_b(b, which, h, acc, ak, w, rgm, rgk, g8)

        def qk_post_b(b, which, h, acc, ak, w, rgm, rgk, g8):
            hb = b % 2
            t0 = blocks[b][1]
            P.op("pe", p_mm(banks[3], blk, sqb[w]), r=["blk", ("sqb", w)], w=[BK(3)])
            P.op("pe", p_mm(banks[4], rgm, qb16[w]), r=[rgk, ("qb16", w)], w=[BK(4)])
            P.op("act", a_act(rstd[w], banks[3], AF.Sqrt, scale=1.0 / 64, bias=EPS), r=[BK(3)], w=[("rstd", w)])
            P.op("dve", lambda e, w=w: e.reciprocal(out=rstd[w], in_=rstd[w]), r=[("rstd", w)], w=[("rstd", w)])
            P.op("dve", v_stt(qn[w], acc, g8, rstd[w], ALU.mult, ALU.mult), r=[ak, "cols", ("rstd", w)], w=[("qn", w)])
            P.op("dve", v_tt(t2[w], banks[4], rstd[w], ALU.mult), r=[BK(4), ("rstd", w)], w=[("t2", w)])
            P.op("dve", v_tt(qn[w], qn[w], rc[hb], ALU.mult), r=[("qn", w), ("rc", hb)], w=[("qn", w)])
            P.op("dve", v_tt(t2[w], t2[w], rs[hb], ALU.mult), r=[("t2", w), ("rs", hb)], w=[("t2", w)])
            o = cnt["q"] % 3
            cnt["q"] += 1
            P.op("dve", v_tt(qo[o], qn[w], t2[w], ALU.add), r=[("qn", w), ("t2", w)], w=[("qo", o)])
            si, c0 = blocks[b][0], segs[blocks[b][0]][0]
            dst = qT[h, :, t0:t0 + 512] if which == "q" else kT_own[h][si][:, t0 - c0:t0 - c0 + 512]
            P.dma("pool", dst, qo[o], r=[("qo", o)])

        stage_A(0)
        prefetch(1)
        si_ = 0
        qk_pend = []
        for b in range(NB):
            si, t0, first, last = blocks[b]
            hb = b % 2
            segbase = 0 if si == 0 else T0 + 2
            segc0 = segs[si][0]
            for s in SLAB_ORDER:
                k = si_ % NSL
                prefetch(si_ + 2)
                si_ += 1
                issue_conv(("rest", l), 1)
                if s == 0:
                    for m in range(4):
                        acc, ak = proj(b, k, True, m)
                        P.op("act", a_act(ax[:, m, :], acc, AF.Copy), r=[ak], w=[("ax", m)])
                elif s == 2:
                    for m in range(4):
                        acc, ak = proj(b, k, True, m)
                        o = cnt["z"] % 2
                        cnt["z"] += 1
                        P.op("dve", v_tt(zst[o], acc, ax[:, m, :], ALU.mult), r=[ak, ("ax", m)], w=[("zst", o)])
                        c = segbase + 1 + (t0 - segc0)
                        P.dma("pool", zT[m * 128:(m + 1) * 128, c:c + 512], zst[o], r=[("zst", o)])
                        if first:
                            P.dma("pool", zb_own[2 * si, m * 128:(m + 1) * 128].rearrange("(d o) -> d o", o=1), zst[o][:, 0:1], r=[("zst", o)])
                        if last:
                            P.dma("pool", zb_own[2 * si + 1, m * 128:(m + 1) * 128].rearrange("(d o) -> d o", o=1), zst[o][:, 511:512], r=[("zst", o)])
                elif s == 1:
                    for m in range(4):
                        acc, ak = proj(b, k, True, m)
                        o = cnt["ab"] % 2
                        cnt["ab"] += 1
                        P.op("act", a_act(abst[o], acc, AF.Copy), r=[ak], w=[("abst", o)])
                        P.dma("pool", abT[m * 128:(m + 1) * 128, t0:t0 + 512], abst[o], r=[("abst", o)])
                elif s in (3, 4, 5, 6):
                    which = "q" if s < 5 else "k"
                    for m in range(4):
                        h = ((s - 3) % 2) * 4 + m
                        acc, ak = proj(b, k, True, m)
                        if qk_pend:
                            qk_pend.pop(0)()
                        qk_pend.append(qk_post(b, which, h, acc, ak))
                    if s == 6:
                        while qk_pend:
                            qk_pend.pop(0)()
                elif s in (7, 8):
                    for tt in range(4):
                        acc, ak = proj(b, k, False, tt)
                        o = cnt["v"] % 3
                        cnt["v"] += 1
                        P.op("act", a_act(vst[o], acc, AF.Copy), r=[ak], w=[("vst", o)])
                        kt = (t0 - segc0 + tt * 128) // 128
                        for hh in range(4):
                            h = (s - 7) * 4 + hh
                            P.dma("pool", v_own[h][si][:, kt * 128:(kt + 1) * 128], vst[o][:, hh * 128:(hh + 1) * 128], r=[("vst", o)])
                elif s == 9:
                    for m in range(4):
                        acc, ak = proj(b, k, True, m)
                        gelu(acc, ak, ug[:, m, :], ("ug", m), G)
                elif s == 10:
                    for tt in range(4):
                        acc, ak = proj(b, k, False, tt)
                        gelu(acc, ak, vg, "vg", G)
                        P.op("dve", v_tt(vtmp, vg, vg, ALU.mult), r=["vg"], w=["octmp"])
                        P.op("dve", v_red(vss, vtmp.rearrange("p (a b) -> p a b", a=4), ALU.add), r=["octmp"], w=["vss"])
                        P.op("act", a_act(vss, vss, AF.Sqrt, scale=1.0 / 128, bias=EPS), r=["vss"], w=["vss"])
                        P.op("dve", lambda e: e.reciprocal(out=vss, in_=vss), r=["vss"], w=["vss"])
                        for hg in range(4):
                            P.op("dve", v_stt(vn[:, tt, hg * 128:(hg + 1) * 128], vg[:, hg * 128:(hg + 1) * 128], vss[:, hg:hg + 1], gv_bc,
                                              ALU.mult, ALU.mult), r=["vg", "vss", "gv_bc"], w=[("vn", tt)])
                    for hg in range(4):
                        for tt in range(4):
                            P.op("pe", p_mm(banks[5][:, tt * 128:(tt + 1) * 128], vn[:, tt, hg * 128:(hg + 1) * 128], wsT[:, hg, :]),
                                 r=[("vn", tt), "wsT"], w=[BK(5)], signal=(tt == 3))
                        for tt in range(4):
                            P.op("dve", v_tt(octmp[:, tt * 128:(tt + 1) * 128], banks[5][:, tt * 128:(tt + 1) * 128], bsb4[:, hg, :], ALU.add),
                                 r=[BK(5), "bsb4"], w=["octmp"])
                        o = cnt["oc"] % 2
                        cnt["oc"] += 1
                        P.op("dve", v_tt(outc[o], octmp, ug[:, hg, :], ALU.mult), r=["octmp", ("ug", hg)], w=[("outc", o)])
                        P.dma("pool", mixT[1536 + hg * 128:1536 + (hg + 1) * 128, t0:t0 + 512], outc[o], r=[("outc", o)])
                if s == 5 and b + 1 < NB:
                    stage_A(b + 1)

    def gather():
        ccs = P.new_sem("cc")
        n = 0
        pairs = [(zb_own, zb_g)]
        for si in range(2):
            for h in range(NH):
                pairs.append((kT_own[h][si], kT_g[h][si]))
                pairs.append((v_own[h][si], v_g[h][si]))
        for src, dst in pairs:
            n += 1
            P.q["pool"].append(lambda e, src=src, dst=dst: e.collective_compute(
                "AllGather", ALU.bypass, replica_groups=GROUPS, ins=[src.ap().opt()], outs=[dst.ap().opt()]).then_inc(ccs))
        P.q["pool"].append(lambda e: e.wait_ge(ccs, n))
        P.op("pool", lambda e: e.memset(cols[:, 31:32], 0.0), w=["ccdone"])
        P.barrier()
        P.dma("sp", zbg_sb, zb_g.ap(), w=["zbg_sb"])
        for c in range(4):
            P.op("pe", p_mm(banks[7][:, 0:4], zbg_sb[:, c * 128:(c + 1) * 128], sel), r=["zbg_sb", "sel"], w=[BK(7)])
            P.op("dve", v_cp(halo[:, c, :], banks[7][:, 0:4]), r=[BK(7)], w=["halo"])

    def pass2(l):
        A = Arena(arena_t, PASS_BASE)
        SMAX = 4 * T1
        kTs = [A.alloc([128, SMAX], BF16) for _ in range(2)]
        Vs = [A.alloc([128, SMAX // 128, 128], BF16) for _ in range(2)]
        qs = [A.alloc([128, 512], BF16) for _ in range(2)]
        NE = 4
        Es = [A.alloc([128, 1024], BF16) for _ in range(NE)]
        zacc = [[A.alloc([128, 1024], F32) for _ in range(2)] for _ in range(2)]
        ones_f = A.alloc([128, 128], F32)
        rz = A.alloc([128, 512], F32)
        o1 = A.alloc([128, 512], F32)
        o2 = A.alloc([128, 512], F32)
        osq = A.alloc([128, 512], BF16)
        rstd = A.alloc([128, 512], F32)
        obs = [A.alloc([128, 512], BF16) for _ in range(2)]
        Sb = [psA, psB, None]
        NS = 3
        Sv = [psA[:, :], psB[:, :]]
        U1, U2 = banks[6], banks[7]
        P.op("dve", lambda e: e.memset(ones_f, 1.0), w=["ones_f"])

        groups = []
        for si, (c0, tl) in enumerate(segs):
            for h in range(NH):
                for qb in range(tl // 512):
                    groups.append((si, h, qb))
        heads = []
        for g in groups:
            if not heads or heads[-1] != (g[0], g[1]):
                heads.append((g[0], g[1]))
        head_idx = {hd: i for i, hd in enumerate(heads)}

        def load_head(hi):
            si, h = heads[hi]
            c0, tl = segs[si]
            hb = hi % 2
            nkt = tl // 128
            for r in range(4):
                P.dma("sp", kTs[hb][:, r * tl:(r + 1) * tl], kT_g[h][si][r * 128:(r + 1) * 128, :], w=[("kTs", hb)])
                P.dma("sp", Vs[hb][:, r * nkt:(r + 1) * nkt, :], v_g[h][si][r * 128:(r + 1) * 128, :],
                      w=[("Vs", hb)], key=("Vsl", hb))

        iters = []
        for gi, (si, h, qb) in enumerate(groups):
            nk = 4 * segs[si][1] // 128
            for kt in range(nk):
                iters.append((gi, kt, nk))

        sctr = {"n": 0}
        s_of = {}
        e_assume = {}
        za_last = {}

        def s_halves(j):
            if j < 2:
                t = Sv[j]
                return t[:, 0:512], t[:, 512:1024], t
            return banks[4], banks[5], None

        def emit_S(i):
            gi, kt, nk = iters[i]
            si, h, qb = groups[gi]
            hb = head_idx[(si, h)] % 2
            if kt == 0:
                c0, tl = segs[si]
                P.dma("sp", qs[gi % 2], qT[h, :, c0 + qb * 512:c0 + (qb + 1) * 512], w=[("qs", gi % 2)])
            j = sctr["n"] % 2
            sctr["n"] += 1
            s_of[i] = j
            etoks = P.war_tokens([("E", i % NE)])
            P.wait_toks("pe", [t for t in etoks if P.owner.get(id(t[0])) in ("dve", "pool")])
            e_assume[i] = etoks
            lo, hi, _ = s_halves(j)
            P.need("pe", [("kTs", hb), ("qs", gi % 2)])
            P.op("pe", p_mm(lo, kTs[hb][0:64, kt * 128:(kt + 1) * 128], qs[gi % 2][0:64, :], True, True, (0, 0)),
                 r=[("kTs", hb), ("qs", gi % 2)], w=[("S", j)], signal=False, emb=True)
            P.op("pe", p_mm(hi, kTs[hb][64:128, kt * 128:(kt + 1) * 128], qs[gi % 2][64:128, :], True, True, (64, 0)),
                 r=[("kTs", hb), ("qs", gi % 2)], w=[("S", j)], emb=True)

        def emit_rest(i):
            gi, kt, nk = iters[i]
            si, h, qb = groups[gi]
            hb = head_idx[(si, h)] % 2
            ek = i % NE
            j = s_of.pop(i)
            P.assume("act", e_assume.pop(i))
            P.op("act", a_act(Es[ek], s_halves(j)[2], AF.Exp, scale=0.125, bias=col(C_NEGC)), r=[("S", j), "cols"], w=[("E", ek)])
            st, sp_ = (kt == 0), (kt == nk - 1)
            P.need("pe", [("Vs", hb)])
            P.op("pe", p_mm(U1, Vs[hb][:, kt, :], Es[ek][:, 0:512], st, sp_), r=[("Vs", hb), ("E", ek)], w=["U1"], signal=False, emb=True)
            P.op("pe", p_mm(U2, Vs[hb][:, kt, :], Es[ek][:, 512:1024], st, sp_), r=[("Vs", hb), ("E", ek)], w=["U2"], emb=True)
            par = kt % 2
            za = zacc[gi % 2][par]
            zk = ("zacc", gi % 2, par)
            if kt < 2:
                P.op("dve", v_cp(za, Es[ek]), r=[("E", ek)], w=[zk])
            else:
                P.assume("dve", [P.lw[zk]])
                P.op("dve", v_tt(za, za, Es[ek], ALU.add), r=[("E", ek), zk], w=[zk])
            if sp_:
                epilogue(gi)

        def epilogue(gi):
            si, h, qb = groups[gi]
            c0, tl = segs[si]
            za0, za1 = zacc[gi % 2]
            zk0, zk1 = ("zacc", gi % 2, 0), ("zacc", gi % 2, 1)
            AUXA, AUXB = banks[4], banks[5]
            P.op("pe", p_mm(AUXA, ones_f, za0[:, 0:512], True, False), r=["ones_f", zk0, zk1], w=[BK(4)], signal=False)
            P.op("pe", p_mm(AUXA, ones_f, za1[:, 0:512], False, True), r=["ones_f", zk0, zk1], w=[BK(4)])
            P.op("pe", p_mm(AUXB, ones_f, za0[:, 512:1024], True, False), r=["ones_f", zk0, zk1], w=[BK(5)], signal=False)
            P.op("pe", p_mm(AUXB, ones_f, za1[:, 512:1024], False, True), r=["ones_f", zk0, zk1], w=[BK(5)])
            P.op("dve", lambda e: e.reciprocal(out=rz, in_=AUXA), r=[BK(4)], w=["rz"])
            P.op("dve", v_tt(o1, U1, rz, ALU.mult), r=["U1", "rz"], w=["o1"])
            P.op("dve", lambda e: e.reciprocal(out=rz, in_=AUXB), r=[BK(5)], w=["rz"])
            P.op("dve", v_stt(o2, U2, col(C_NEGLAM), rz, ALU.mult, ALU.mult), r=["U2", "rz", "cols"], w=["o2"])
            P.op("dve", v_tt(o1, o1, o2, ALU.add), r=["o1", "o2"], w=["o1"])
            P.op("dve", v_tt(osq, o1, o1, ALU.mult), r=["o1"], w=["osq"])
            P.op("pe", p_mm(AUXA, ones, osq), r=["ones", "osq"], w=[BK(4)])
            P.op("act", a_act(rstd, AUXA, AF.Ln, scale=1.0 / 128, bias=EPS), r=[BK(4)], w=["rstd2"])
            P.op("act", a_act(rstd, rstd, AF.Exp, scale=-0.5), r=["rstd2"], w=["rstd2"])
            ob = gi % 2
            P.op("dve", v_stt(obs[ob], o1, col(C_GSUB), rstd, ALU.mult, ALU.mult), r=["o1", "rstd2", "cols"], w=[("obs", ob)])
            P.dma("pool", mixT[512 + h * 128:512 + (h + 1) * 128, c0 + qb * 512:c0 + (qb + 1) * 512], obs[ob], r=[("obs", ob)])

        load_head(0)
        if len(heads) > 1:
            load_head(1)
        loaded = 2
        emit_S(0)
        for i in range(len(iters)):
            if i + 1 < len(iters):
                emit_S(i + 1)
            emit_rest(i)
            gi, kt, nk = iters[i]
            if kt == nk - 1 and (gi + 1 == len(groups) or groups[gi + 1][:2] != groups[gi][:2]):
                if loaded < len(heads):
                    load_head(loaded)
                    loaded += 1

    def pass3(l, xsrc, xdst):
        A = Arena(arena_t, PASS_BASE)
        g2bc = A.alloc([128, D], F32)
        xb = A.alloc([128, 4, D], F32)
        mix = [A.alloc([128, 16, 512], BF16) for _ in range(1)]
        h2 = A.alloc([128, D], BF16)
        h2T = mix[0]
        act = A.alloc([128, 64, 512], BF16)
        NSL = 3
        wsl = [A.alloc([128, 16, 512], BF16) for _ in range(NSL)]
        zw = [A.alloc([128, 514], F32) for _ in range(2)]
        abw = [A.alloc([128, 512], F32) for _ in range(2)]
        cacc = A.alloc([128, 512], F32)
        rl = [A.alloc([128, 512], F32) for _ in range(2)]
        ssq = A.alloc([128, 8], F32)

        P.dma("sp", g2bc, norm2_g[l].partition_broadcast(128), w=["g2bc"])
        wo = wb["out"][l].rearrange("(kc p) c -> p kc c", p=128)
        wu = wb["up"][l].rearrange("(kc p) c -> p kc c", p=128)
        wd = wb["down"][l].rearrange("(kc p) c -> p kc c", p=128)
        seq = []
        for b in range(NB):
            seq += [("out", n) for n in range(4)] + [("up", s) for s in range(16)] + [("down", n, s) for n in range(4) for s in range(4)]
        wn = {"n": 0}

        def prefetch(upto):
            while wn["n"] <= upto and wn["n"] < len(seq):
                i = wn["n"]
                wn["n"] += 1
                k = i % NSL
                it = seq[i]
                if it[0] == "out":
                    src, key = wo[:, :, it[1] * 512:(it[1] + 1) * 512], ("wb", "out", l, it[1])
                elif it[0] == "up":
                    src, key = wu[:, :, it[1] * 512:(it[1] + 1) * 512], ("wb", "up", l, it[1])
                else:
                    src, key = wd[:, it[2] * 16:(it[2] + 1) * 16, it[1] * 512:(it[1] + 1) * 512], ("wb", "down", l, it[1], it[2])
                P.dma("sp", wsl[k], src, r=[key], w=[("wsl", k)])

        wi = 0
        issue_conv(("rest", l))
        prefetch(1)
        accn = 0
        cn = 0
        for b in range(NB):
            si, t0, first, last = blocks[b]
            segbase = 0 if si == 0 else T0 + 2
            segc0 = segs[si][0]
            mx = mix[0]
            for tt in range(4):
                P.dma("sp", xb[:, tt, :], xsrc[t0 + tt * 128:t0 + (tt + 1) * 128, :], w=[("xb", tt)])
            P.dma("sp", mx[:, 4:16, :], mixT[512:2048, t0:t0 + 512].rearrange("(c p) t -> p c t", p=128), w=["mix"])
            for c in range(4):
                o = cn % 2
                cn += 1
                cz = segbase + (t0 - segc0)
                P.dma("sp", zw[o], zT[c * 128:(c + 1) * 128, cz:cz + 514], w=[("zw", o)])
                P.dma("sp", abw[o], abT[c * 128:(c + 1) * 128, t0:t0 + 512], w=[("abw", o)])
                if first:
                    P.op("dve", v_cp(zw[o][:, 0:1], halo[:, c, 2 * si:2 * si + 1]), r=["halo"], w=[("zw", o)])
                if last:
                    P.op("dve", v_cp(zw[o][:, 513:514], halo[:, c, 2 * si + 1:2 * si + 2]), r=["halo"], w=[("zw", o)])
                P.op("dve", v_ts(cacc, zw[o][:, 1:513], convw[:, c, 1:2], None, ALU.mult), r=[("zw", o), "convw"], w=["cacc"])
                P.op("dve", v_stt(cacc, zw[o][:, 0:512], convw[:, c, 0:1], cacc, ALU.mult, ALU.add), r=[("zw", o), "convw", "cacc"], w=["cacc"])
                P.op("dve", v_stt(cacc, zw[o][:, 2:514], convw[:, c, 2:3], cacc, ALU.mult, ALU.add), r=[("zw", o), "convw", "cacc"], w=["cacc"])
                P.op("dve", v_tt(mx[:, c, :], cacc, abw[o], ALU.mult), r=["cacc", ("abw", o)], w=["mix"])
            for n in range(4):
                k = wi % NSL
                prefetch(wi + 2)
                wi += 1
                for tt in range(4):
                    a = accn % 4
                    accn += 1
                    for kc in range(16):
                        P.op("pe", p_mm(banks[a], mx[:, kc, tt * 128:(tt + 1) * 128], wsl[k][:, kc, :], kc == 0, kc == 15),
                             r=["mix", ("wsl", k)], w=[BK(a)], signal=(kc == 15))
                    P.op("dve", v_tt(xb[:, tt, n * 512:(n + 1) * 512], banks[a], xb[:, tt, n * 512:(n + 1) * 512], ALU.add),
                         r=[BK(a), ("xb", tt)], w=[("xb", tt)])
            for tt in range(4):
                sc = ssq[:, tt:tt + 1]
                P.op("act", lambda e, tt=tt, sc=sc: e.activation(out=h2, in_=xb[:, tt, :], func=AF.Square, accum_out=sc),
                     r=[("xb", tt)], w=["h2", "ssq3"], emb=False)
                P.op("act", a_act(sc, sc, AF.Sqrt, scale=1.0 / D, bias=EPS), r=["ssq3"], w=["ssq3"])
                P.op("dve", lambda e, sc=sc: e.reciprocal(out=sc, in_=sc), r=["ssq3"], w=["ssq3"])
                P.op("dve", v_stt(h2, xb[:, tt, :], sc, g2bc, ALU.mult, ALU.mult), r=[("xb", tt), "ssq3", "g2bc"], w=["h2"])
                for g4 in range(4):
                    bi = 6 + (g4 % 2)
                    pt = banks[bi].bitcast(BF16)
                    for q in range(4):
                        fc = g4 * 4 + q
                        P.op("pe", p_tr(pt[:, q * 128:(q + 1) * 128], h2[:, fc * 128:(fc + 1) * 128], ident),
                             r=["h2", "ident"], w=[BK(bi)], signal=(q == 3))
                    P.op("act", a_act(h2T[:, g4 * 4:(g4 + 1) * 4, tt * 128:(tt + 1) * 128],
                                      pt[:, 0:512].rearrange("p (a b) -> p a b", a=4), AF.Copy), r=[BK(bi)], w=["mix"])
            for s in range(16):
                k = wi % NSL
                prefetch(wi + 2)
                wi += 1
                issue_conv(("in", l + 1), 1)
                for m in range(4):
                    a = accn % 4
                    accn += 1
                    for kc in range(16):
                        P.op("pe", p_mm(banks[a], wsl[k][:, kc, m * 128:(m + 1) * 128], h2T[:, kc, :], kc == 0, kc == 15),
                             r=["mix", ("wsl", k)], w=[BK(a)], signal=(kc == 15))
                    ro = accn % 2
                    P.op("act", a_act(rl[ro], banks[a], AF.Relu), r=[BK(a)], w=[("rl", ro)])
                    P.op("pool", v_tt(act[:, s * 4 + m, :], rl[ro], rl[ro], ALU.mult), r=[("rl", ro)], w=["act"])
            for n in range(4):
                base = 0 if n % 2 == 0 else 4
                for s in range(4):
                    k = wi % NSL
                    prefetch(wi + 2)
                    wi += 1
                    for fc in range(16):
                        for tt in range(4):
                            lastmm = (s == 3 and fc == 15)
                            P.op("pe", p_mm(banks[base + tt], act[:, s * 16 + fc, tt * 128:(tt + 1) * 128], wsl[k][:, fc, :],
                                            s == 0 and fc == 0, lastmm),
                                 r=["act", ("wsl", k)], w=[BK(base + tt)], signal=(lastmm or (fc == 15 and tt == 3)))
                for tt in range(4):
                    P.op("dve", v_tt(xb[:, tt, n * 512:(n + 1) * 512], banks[base + tt], xb[:, tt, n * 512:(n + 1) * 512], ALU.add),
                         r=[BK(base + tt), ("xb", tt)], w=[("xb", tt)])
            for tt in range(4):
                P.dma("pool", xdst[t0 + tt * 128:t0 + (tt + 1) * 128, :], xb[:, tt, :], r=[("xb", tt)])

    for l in range(depth):
        xsrc = x_in if l == 0 else x1
        xdst = y_out if l == depth - 1 else x1
        layer_consts(l)
        issue_conv(("in", l))
        pass1(l, xsrc)
        P.barrier()
        if stop_after == ("p1", l):
            break
        gather()
        if stop_after == ("g", l):
            break
        pass2(l)
        P.barrier()
        if stop_after == ("p2", l):
            break
        pass3(l, xsrc, xdst)
        P.barrier()

    with nc.Block() as block:
        @block.tensor
        def _(e):
            for f in P.q["pe"]:
                f(e)

        @block.scalar
        def _(e):
            for f in P.q["act"]:
                f(e)

        @block.vector
        def _(e):
            for f in P.q["dve"]:
                f(e)

        @block.gpsimd
        def _(e):
            for f in P.q["pool"]:
                f(e)

        @block.sync
        def _(e):
            for f in P.q["sp"]:
                f(e)
    stack.close()
    print("instr counts", {k: len(v) for k, v in P.q.items()}, "nsem", P.nsem, flush=True)
    return nc


def host_consts(T0, T1, core):
    r = core % 4
    pos = np.concatenate([r * T0 + np.arange(T0), r * T1 + np.arange(T1)]).astype(np.float32)
    inv = (ROPE_THETA ** (-np.arange(0, 16, 2, dtype=np.float32) / 16)).astype(np.float32)
    ang = pos[:, None] * inv[None, :]
    cs, sn = np.cos(ang).astype(np.float32), np.sin(ang).astype(np.float32)
    T = T0 + T1
    C = np.ones((128, T), np.float32)
    S = np.zeros((128, T), np.float32)
    for gb in (0, 64):
        for d in range(16):
            C[gb + d] = cs[:, d % 8]
            S[gb + d] = sn[:, d % 8]
    ident = np.eye(128, dtype=np.float32)
    blk = np.zeros((128, 128), np.float32)
    blk[:64, :64] = 1
    blk[64:, 64:] = 1
    rmat = np.zeros((128, 128), np.float32)
    for gb in (0, 64):
        for m in range(8):
            rmat[gb + m + 8, gb + m] = -1.0
        for m in range(8, 16):
            rmat[gb + m - 8, gb + m] = 1.0
    sel = np.zeros((16, 4), np.float32)
    for s in range(2):
        if r > 0:
            sel[(r - 1) * 4 + 2 * s + 1, 2 * s] = 1.0
        if r < 3:
            sel[(r + 1) * 4 + 2 * s, 2 * s + 1] = 1.0
    selz = np.zeros((64, 256), np.float32)
    selz[0, 0:128] = 1.0
    selz[32, 128:256] = 1.0
    return {"ropec": C, "ropes": S, "c_ident": ident, "c_blk": blk, "c_rmat": rmat, "c_sel": sel, "c_selz": selz}


_CACHE = {}


def run(inputs, T0, T1, depth=2, stop_after=None, trace=False):
    key = (T0, T1, depth, stop_after)
    if key not in _CACHE:
        _CACHE[key] = build(T0, T1, depth, stop_after)
    nc = _CACHE[key]
    xp = np.asarray(inputs["x_prompt"], np.float32)
    xs = np.asarray(inputs["x_sample"], np.float32)
    shared = {k: np.ascontiguousarray(np.asarray(inputs[k], np.float32)) for k in (
        "w_in", "w_out", "w_up", "w_down", "norm1_g", "norm2_g", "conv_w", "q_norm_g", "k_norm_g",
        "lam_q1", "lam_k1", "lam_q2", "lam_k2", "subln_g", "sgu_norm_g", "sgu_w", "sgu_b")}
    in_maps = []
    for c in range(NCORES):
        g, r = c // 4, c % 4
        m = dict(shared)
        m["x_in"] = np.ascontiguousarray(np.concatenate([xp[g, r * T0:(r + 1) * T0], xs[g, r * T1:(r + 1) * T1]], 0))
        m.update(host_consts(T0, T1, c))
        in_maps.append(m)
    res = run_bass_kernel_spmd(nc, in_maps, core_ids=list(range(NCORES)), **({"trace": True} if trace else {}))
    yp = np.zeros((2, 4 * T0, D), np.float32)
    ys = np.zeros((2, 4 * T1, D), np.float32)
    for c in range(NCORES):
        g, r = c // 4, c % 4
        y = np.asarray(res.results[c]["y_out"], np.float32)
        yp[g, r * T0:(r + 1) * T0] = y[:T0]
        ys[g, r * T1:(r + 1) * T1] = y[T0:]
    return (yp, ys), res


def kernel(**inputs):
    (yp, ys), _ = run(inputs, 1024, 4096, 2)
    return (yp, ys)
```

```python
import math
from contextlib import ExitStack

import numpy as np
import concourse.bass as bass
import concourse.mybir as mybir
from concourse.bass_utils import run_bass_kernel_spmd

F32 = mybir.dt.float32
BF16 = mybir.dt.bfloat16
U8 = mybir.dt.uint8
AF = mybir.ActivationFunctionType
ALU = mybir.AluOpType
AX = mybir.AxisListType

D = 2048
DIN = 5632
DFF = 8192
NH = 8
EPS = 1e-6
ROPE_THETA = 500000.0
NCORES = 8
GROUPS = [[0, 1, 2, 3], [4, 5, 6, 7]]
ENG = ("pe", "act", "dve", "pool", "sp")
ARENA_BYTES = 206 * 1024
SEM_ROT = 30000
EMBED_WAIT = False
SKIP_SAME_ENGINE = False


class Slot:
    def __init__(self, sem):
        self.sem = sem
        self.cnt = 0


class Sched:
    def __init__(self, nc, stack):
        self.nc = nc
        self.stack = stack
        self.q = {e: [] for e in ENG}
        self.sem = {}
        self.cnt = {}
        self.nsem = 0
        self.waited = {e: {} for e in ENG}
        self.last = {e: None for e in ENG}
        self.lw = {}
        self.lr = {}
        self.pend = {e: ([], []) for e in ENG}
        self.slots = {}
        self.owner = {}
        self.n_instr = 0
        for e in ENG:
            self._rot(e)

    def new_sem(self, name):
        self.nsem += 1
        return self.stack.enter_context(self.nc.semaphore(f"{name}{self.nsem}"))

    def _rot(self, e):
        self.sem[e] = self.new_sem("s" + e)
        self.owner[id(self.sem[e])] = e
        self.cnt[e] = 0

    def _need(self, eng, tok, out):
        if tok is None:
            return
        sem, val = tok
        w = self.waited[eng]
        if w.get(id(sem), 0) >= val:
            return
        w[id(sem)] = val
        out[id(sem)] = (sem, val)

    def _wait(self, eng, tok):
        out = {}
        self._need(eng, tok, out)
        for sem, val in out.values():
            self.q[eng].append(lambda e, sem=sem, val=val: e.wait_ge(sem, val))

    def _deps(self, eng, r, w):
        out = {}
        own = self.owner
        for k in r:
            self._need(eng, self.lw.get(k), out)
        for k in w:
            t = self.lw.get(k)
            if t is not None and (not SKIP_SAME_ENGINE or own.get(id(t[0])) != eng):
                self._need(eng, t, out)
            for t in self.lr.get(k, ()):
                if not SKIP_SAME_ENGINE or own.get(id(t[0])) != eng:
                    self._need(eng, t, out)
        return list(out.values())

    def war_tokens(self, keys):
        ts = []
        for k in keys:
            if self.lw.get(k) is not None:
                ts.append(self.lw[k])
            ts.extend(self.lr.get(k, ()))
        return ts

    def wait_toks(self, eng, toks):
        for t in toks:
            self._wait(eng, t)

    def assume(self, eng, toks):
        w = self.waited[eng]
        for sem, val in toks:
            if w.get(id(sem), 0) < val:
                w[id(sem)] = val

    def need(self, eng, keys):
        for sem, val in self._deps(eng, keys, ()):
            self.q[eng].append(lambda e, sem=sem, val=val: e.wait_ge(sem, val))

    def _reg(self, tok, r, w):
        for k in r:
            self.lr.setdefault(k, []).append(tok)
        for k in w:
            self.lw[k] = tok
            self.lr[k] = []

    def _emit(self, eng, fn, toks, emb, inc):
        emb_tok = toks.pop() if (emb and EMBED_WAIT and toks) else None
        for sem, val in toks:
            self.q[eng].append(lambda e, sem=sem, val=val: e.wait_ge(sem, val))

        def run(e, fn=fn, emb_tok=emb_tok, inc=inc):
            ins = fn(e)
            if emb_tok is not None:
                ins._wait_ge(emb_tok[0], emb_tok[1])
            if inc is not None:
                ins.then_inc(inc[0], inc[1])
        self.q[eng].append(run)

    def op(self, eng, fn, r=(), w=(), signal=True, emb=None):
        toks = self._deps(eng, r, w)
        if emb is None:
            emb = eng != "pe"
        self.n_instr += 1
        if not signal:
            self.pend[eng][0].extend(r)
            self.pend[eng][1].extend(w)
            self._emit(eng, fn, toks, emb, None)
            return None
        if self.cnt[eng] >= SEM_ROT:
            self._rot(eng)
        self.cnt[eng] += 1
        sem, val = self.sem[eng], self.cnt[eng]
        self._emit(eng, fn, toks, emb, (sem, 1))
        tok = (sem, val)
        self.last[eng] = tok
        pr, pw = self.pend[eng]
        self._reg(tok, list(r) + pr, list(w) + pw)
        self.pend[eng] = ([], [])
        return tok

    def dma(self, eng, out, in_, r=(), w=(), key=None, **kw):
        toks = self._deps(eng, r, w)
        self.n_instr += 1
        if key is None:
            key = (tuple(w) + tuple(r))[0]
        if key not in self.slots:
            self.slots[key] = Slot(self.new_sem("d"))
        s = self.slots[key]
        s.cnt += 16
        self._emit(eng, lambda e, out=out, in_=in_, kw=kw: e.dma_start(out=out, in_=in_, **kw), toks, False, (s.sem, 16))
        tok = (s.sem, s.cnt)
        self._reg(tok, r, w)
        return tok

    def barrier(self):
        ts = [self.last[e] for e in ("pe", "act", "dve", "pool")]
        ts += [(s.sem, s.cnt) for s in self.slots.values() if s.cnt]
        for e in ENG:
            for t in ts:
                self._wait(e, t)


class Arena:
    def __init__(self, buf, base=0, limit=ARENA_BYTES):
        self.buf = buf
        self.off = base
        self.limit = limit

    def alloc(self, shape, dtype):
        esz = 4 if dtype == F32 else 2
        n = int(np.prod(shape[1:])) * esz
        off = (self.off + 63) // 64 * 64
        assert off + n <= self.limit, f"arena overflow {off}+{n}>{self.limit}"
        self.off = off + n
        ap = self.buf[0:shape[0], off:off + n].bitcast(dtype)
        if len(shape) == 3:
            ap = ap.rearrange("p (a b) -> p a b", a=shape[1])
        return ap


def v_ts(out, in0, s1, s2, op0, op1=None):
    if op1 is None:
        return lambda e: e.tensor_scalar(out=out, in0=in0, scalar1=s1, scalar2=s2, op0=op0)
    return lambda e: e.tensor_scalar(out=out, in0=in0, scalar1=s1, scalar2=s2, op0=op0, op1=op1)


def v_tt(out, a, b, op):
    return lambda e: e.tensor_tensor(out=out, in0=a, in1=b, op=op)


def v_stt(out, in0, sc, in1, op0, op1):
    return lambda e: e.scalar_tensor_tensor(out=out, in0=in0, scalar=sc, in1=in1, op0=op0, op1=op1)


def v_cp(out, in_):
    return lambda e: e.tensor_copy(out=out, in_=in_)


def v_red(out, in_, op, absv=False):
    if absv:
        return lambda e: e.tensor_reduce(out=out, in_=in_, axis=AX.X, op=op, apply_absolute_value=True)
    return lambda e: e.tensor_reduce(out=out, in_=in_, axis=AX.X, op=op)


def a_act(out, in_, func, scale=None, bias=None):
    kw = {}
    if scale is not None:
        kw["scale"] = scale
    if bias is not None:
        kw["bias"] = bias
    return lambda e: e.activation(out=out, in_=in_, func=func, **kw)


def p_mm(out, lhsT, rhs, start=True, stop=True, tp=None):
    if tp is None:
        return lambda e: e.matmul(out, lhsT=lhsT, rhs=rhs, start=start, stop=stop)
    return lambda e: e.matmul(out, lhsT=lhsT, rhs=rhs, start=start, stop=stop, tile_position=tp)


def p_tr(out, in_, ident):
    return lambda e: e.transpose(out=out, in_=in_, identity=ident)


def build(T0, T1, depth=2, stop_after=None):
    T = T0 + T1
    segs = [(0, T0), (T0, T1)]
    NKT = T // 128
    nc = bass.Bass("TRN2", target_bir_lowering=False)
    stack = ExitStack()

    def din(name, shape, dt=F32):
        return nc.dram_tensor(name, list(shape), dt, kind="ExternalInput")

    x_in = din("x_in", [T, D])
    w_in = din("w_in", [depth, D, DIN])
    w_out = din("w_out", [depth, D, D])
    w_up = din("w_up", [depth, D, DFF])
    w_down = din("w_down", [depth, DFF, D])
    norm1_g = din("norm1_g", [depth, D])
    norm2_g = din("norm2_g", [depth, D])
    conv_w = din("conv_w", [depth, 3, 512])
    q_norm_g = din("q_norm_g", [depth, 64])
    k_norm_g = din("k_norm_g", [depth, 64])
    lam_q1 = din("lam_q1", [depth, 64])
    lam_k1 = din("lam_k1", [depth, 64])
    lam_q2 = din("lam_q2", [depth, 64])
    lam_k2 = din("lam_k2", [depth, 64])
    subln_g = din("subln_g", [depth, 128])
    sgu_norm_g = din("sgu_norm_g", [depth, 128])
    sgu_w = din("sgu_w", [depth, 4, 128, 128])
    sgu_b = din("sgu_b", [depth, 4, 128])
    ropec = din("ropec", [128, T])
    ropes = din("ropes", [128, T])
    c_ident = din("c_ident", [128, 128])
    c_blk = din("c_blk", [128, 128])
    c_rmat = din("c_rmat", [128, 128])
    c_sel = din("c_sel", [16, 4])
    c_selz = din("c_selz", [64, 256])
    y_out = nc.dram_tensor("y_out", [T, D], F32, kind="ExternalOutput")

    wb = {"in": nc.dram_tensor("wb_in", [depth, D, DIN], BF16), "out": nc.dram_tensor("wb_out", [depth, D, D], BF16),
          "up": nc.dram_tensor("wb_up", [depth, D, DFF], BF16), "down": nc.dram_tensor("wb_down", [depth, DFF, D], BF16)}
    wsrc = {"in": w_in, "out": w_out, "up": w_up, "down": w_down}
    x1 = nc.dram_tensor("x1", [T, D], F32)
    qT = nc.dram_tensor("qT", [NH, 128, T], BF16)
    kT_own = [[nc.dram_tensor(f"kT_own_{h}_{si}", [128, tl], BF16) for si, (c0, tl) in enumerate(segs)] for h in range(NH)]
    kT_g = [[nc.dram_tensor(f"kT_g_{h}_{si}", [512, tl], BF16) for si, (c0, tl) in enumerate(segs)] for h in range(NH)]
    v_own = [[nc.dram_tensor(f"v_own_{h}_{si}", [128, tl], BF16) for si, (c0, tl) in enumerate(segs)] for h in range(NH)]
    v_g = [[nc.dram_tensor(f"v_g_{h}_{si}", [512, tl], BF16) for si, (c0, tl) in enumerate(segs)] for h in range(NH)]
    zb_own = nc.dram_tensor("zb_own", [4, 512], F32)
    zb_g = nc.dram_tensor("zb_g", [16, 512], F32)
    zT = nc.dram_tensor("zT", [512, T + 4], F32)
    abT = nc.dram_tensor("abT", [512, T], F32)
    mixT = nc.dram_tensor("mixT", [D, T], BF16)

    arena_t = stack.enter_context(nc.sbuf_tensor("arena", [128, ARENA_BYTES], U8))
    psA = stack.enter_context(nc.psum_tensor("psA", [128, 1024], F32))
    psB = stack.enter_context(nc.psum_tensor("psB", [128, 1024], F32))
    ps4 = [stack.enter_context(nc.psum_tensor(f"ps{i}", [128, 512], F32)) for i in range(4)]
    banks = [psA[:, 0:512], psA[:, 512:1024], psB[:, 0:512], psB[:, 512:1024]] + [p[:, :] for p in ps4]

    def BK(i):
        return ("bank", i)

    P = Sched(nc, stack)

    CA = Arena(arena_t, 0, 12 * 1024)
    ident_f = CA.alloc([128, 128], F32)
    ident = CA.alloc([128, 128], BF16)
    blk = CA.alloc([128, 128], BF16)
    ones = CA.alloc([128, 128], BF16)
    rmat = CA.alloc([128, 128], F32)
    rg_q = CA.alloc([128, 128], BF16)
    rg_k = CA.alloc([128, 128], BF16)
    selz = CA.alloc([64, 256], F32)
    sel = CA.alloc([16, 4], F32)
    gq_bc = CA.alloc([128, 64], F32)
    gk_bc = CA.alloc([128, 64], F32)
    lam_bc = CA.alloc([128, 4, 64], F32)
    lam_tmp = CA.alloc([128, 64], F32)
    cols = CA.alloc([128, 32], F32)
    convw = CA.alloc([128, 4, 3], F32)
    halo = CA.alloc([128, 4, 4], F32)
    zbg_sb = CA.alloc([16, 512], F32)
    PASS_BASE = 12 * 1024
    C_G8Q, C_G8K, C_NEGC, C_MQ, C_MK, C_S1, C_S2, C_LAM, C_NEGLAM, C_GSUB, C_GQ, C_GK, C_SUB0 = range(13)

    def col(i):
        return cols[:, i:i + 1]

    tmp_f = CA.alloc([128, 128], F32)
    P.dma("sp", ident_f, c_ident.ap(), w=["ident_f"])
    P.dma("sp", rmat, c_rmat.ap(), w=["rmat"])
    P.dma("sp", selz, c_selz.ap(), w=["selz"])
    P.dma("sp", sel, c_sel.ap(), w=["sel"])
    P.dma("sp", tmp_f, c_blk.ap(), w=["tmp_f"])
    P.op("dve", v_cp(ident, ident_f), r=["ident_f"], w=["ident"])
    P.op("dve", v_cp(blk, tmp_f), r=["tmp_f"], w=["blk"])
    P.op("dve", lambda e: e.memset(ones, 1.0), w=["ones"])

    wcn = {"n": 0}

    def wckey():
        wcn["n"] += 1
        return ("wc", wcn["n"] % 6)

    SLAB_ORDER = [0, 2, 1, 3, 4, 5, 6, 7, 8, 9, 10]
    conv_q = {}

    def mk_conv(dst, src_, key):
        return lambda: P.dma("pool", dst, src_, w=[key], key=wckey())

    for l in range(depth):
        qi = []
        for s in SLAB_ORDER:
            qi.append(mk_conv(wb["in"][l, :, s * 512:(s + 1) * 512], wsrc["in"][l, :, s * 512:(s + 1) * 512], ("wb", "in", l, s)))
        conv_q[("in", l)] = qi
        qo_ = []
        for n in range(4):
            qo_.append(mk_conv(wb["out"][l, :, n * 512:(n + 1) * 512], wsrc["out"][l, :, n * 512:(n + 1) * 512], ("wb", "out", l, n)))
        for s in range(16):
            qo_.append(mk_conv(wb["up"][l, :, s * 512:(s + 1) * 512], wsrc["up"][l, :, s * 512:(s + 1) * 512], ("wb", "up", l, s)))
        for n in range(4):
            for s in range(4):
                qo_.append(mk_conv(wb["down"][l, s * 2048:(s + 1) * 2048, n * 512:(n + 1) * 512],
                                   wsrc["down"][l, s * 2048:(s + 1) * 2048, n * 512:(n + 1) * 512], ("wb", "down", l, n, s)))
        conv_q[("rest", l)] = qo_

    def issue_conv(which, n=None):
        q = conv_q.get(which, [])
        k = len(q) if n is None else min(n, len(q))
        for _ in range(k):
            q.pop(0)()

    issue_conv(("in", 0))

    def layer_consts(l):
        lambda_init = 0.8 - 0.6 * math.exp(-0.3 * l)
        lck = []

        def lc(dst, src_):
            k = ("lc", len(lck))
            lck.append(k)
            P.dma("sp", dst, src_, w=[k], key="lcslot")

        lc(gq_bc, q_norm_g[l].partition_broadcast(128))
        lc(gk_bc, k_norm_g[l].partition_broadcast(128))
        for i, lm in enumerate((lam_q1, lam_k1, lam_q2, lam_k2)):
            lc(lam_bc[:, i, :], lm[l].partition_broadcast(128))
        for half in range(2):
            lc(cols[half * 64:(half + 1) * 64, C_GQ:C_GQ + 1], q_norm_g[l].rearrange("(d o) -> d o", o=1))
            lc(cols[half * 64:(half + 1) * 64, C_GK:C_GK + 1], k_norm_g[l].rearrange("(d o) -> d o", o=1))
        lc(col(C_SUB0), subln_g[l].rearrange("(d o) -> d o", o=1))
        for k in range(3):
            for c in range(4):
                lc(convw[:, c, k:k + 1], conv_w[l, k, c * 128:(c + 1) * 128].rearrange("(d o) -> d o", o=1))
        P.op("dve", lambda e: e.memset(cols[:, 30:31], 0.0), r=lck, w=["cols", "gq_bc", "gk_bc", "lam_bc", "convw"])
        CW = dict(r=["cols"], w=["cols"])
        P.op("dve", v_ts(col(C_G8Q), col(C_GQ), 1.0, None, ALU.mult), **CW)
        P.op("dve", v_ts(col(C_G8K), col(C_GK), 1.0, None, ALU.mult), **CW)
        P.op("dve", v_ts(rg_q, rmat, col(C_G8Q), None, ALU.mult), r=["cols", "rmat"], w=["rg_q"])
        P.op("dve", v_ts(rg_k, rmat, col(C_G8K), None, ALU.mult), r=["cols", "rmat"], w=["rg_k"])
        P.op("dve", v_red(col(C_MQ), gq_bc, ALU.max, True), r=["gq_bc", "cols"], w=["cols"])
        P.op("dve", v_red(col(C_MK), gk_bc, ALU.max, True), r=["gk_bc", "cols"], w=["cols"])
        P.op("dve", v_ts(col(C_NEGC), col(C_MQ), col(C_MK), -8.0, ALU.mult, ALU.mult), **CW)
        P.op("dve", v_tt(lam_tmp, lam_bc[:, 0, :], lam_bc[:, 1, :], ALU.mult), r=["lam_bc"], w=["lam_tmp"])
        P.op("dve", v_red(col(C_S1), lam_tmp, ALU.add), r=["lam_tmp", "cols"], w=["cols"])
        P.op("dve", v_tt(lam_tmp, lam_bc[:, 2, :], lam_bc[:, 3, :], ALU.mult), r=["lam_bc"], w=["lam_tmp"])
        P.op("dve", v_red(col(C_S2), lam_tmp, ALU.add), r=["lam_tmp", "cols"], w=["cols"])
        P.op("act", a_act(cols[:, C_S1:C_S2 + 1], cols[:, C_S1:C_S2 + 1], AF.Exp), **CW)
        P.op("dve", v_ts(col(C_LAM), col(C_S1), col(C_S2), lambda_init, ALU.subtract, ALU.add), **CW)
        P.op("dve", v_ts(col(C_NEGLAM), col(C_LAM), -1.0, None, ALU.mult), **CW)
        P.op("dve", v_ts(col(C_GSUB), col(C_SUB0), 1.0 - lambda_init, None, ALU.mult), **CW)

    def mk_blocks():
        blocks = []
        for si, (c0, tl) in enumerate(segs):
            for j in range(tl // 512):
                blocks.append((si, c0 + j * 512, j == 0, j == tl // 512 - 1))
        return blocks

    blocks = mk_blocks()
    NB = len(blocks)

    def gelu(acc, acc_key, out_ap, out_key, G):
        g_sq, g_xh, g_u, g_th = G
        P.op("act", a_act(g_sq, acc, AF.Square), r=[acc_key], w=["g_sq"])
        P.op("act", a_act(g_xh, acc, AF.Copy, scale=0.5), r=[acc_key], w=["g_xh"])
        P.op("dve", v_ts(g_u, g_sq, 0.044715, 1.0, ALU.mult, ALU.add), r=["g_sq"], w=["g_u"])
        P.op("dve", v_tt(g_u, g_u, g_xh, ALU.mult), r=["g_u", "g_xh"], w=["g_u"])
        P.op("act", a_act(g_th, g_u, AF.Tanh, scale=2.0 * 0.7978845608028654), r=["g_u"], w=["g_th"])
        P.op("dve", v_stt(out_ap, g_th, 1.0, g_xh, ALU.add, ALU.mult), r=["g_th", "g_xh"], w=[out_key])

    def pass1(l, xsrc):
        A = Arena(arena_t, PASS_BASE)
        g1bc = A.alloc([128, D], F32)
        xt = [A.alloc([128, D], F32) for _ in range(1)]
        ht = [A.alloc([128, D], BF16) for _ in range(1)]
        hT = [A.alloc([128, 16, 512], BF16) for _ in range(2)]
        NSL = 3
        wsl = [A.alloc([128, 16, 512], BF16) for _ in range(NSL)]
        ax = A.alloc([128, 4, 512], F32)
        zst = [A.alloc([128, 512], F32) for _ in range(2)]
        abst = [A.alloc([128, 512], F32) for _ in range(2)]
        rc = [A.alloc([128, 512], F32) for _ in range(2)]
        rs = [A.alloc([128, 512], F32) for _ in range(2)]
        sqb = [A.alloc([128, 512], BF16) for _ in range(2)]
        qb16 = [A.alloc([128, 512], BF16) for _ in range(2)]
        rstd = [A.alloc([128, 512], F32) for _ in range(2)]
        qn = [A.alloc([128, 512], F32) for _ in range(2)]
        t2 = [A.alloc([128, 512], F32) for _ in range(2)]
        qo = [A.alloc([128, 512], BF16) for _ in range(3)]
        vst = [A.alloc([128, 512], BF16) for _ in range(3)]
        ug = A.alloc([128, 4, 512], F32)
        G = [A.alloc([128, 512], F32) for _ in range(4)]
        vg = A.alloc([128, 512], F32)
        vss = A.alloc([128, 4], F32)
        vn = A.alloc([128, 4, 512], BF16)
        outc = [A.alloc([128, 512], BF16) for _ in range(2)]
        octmp = A.alloc([128, 512], F32)
        vtmp = octmp
        bsb4 = A.alloc([128, 4, 128], F32)
        gv_bc = A.alloc([128, 128], F32)
        ws_f = A.alloc([128, 4, 128], F32)
        ws_b = A.alloc([128, 4, 128], BF16)
        wsT = A.alloc([128, 4, 128], BF16)
        ssq = A.alloc([128, 8], F32)

        P.dma("sp", g1bc, norm1_g[l].partition_broadcast(128), w=["g1bc"])
        P.dma("sp", gv_bc, sgu_norm_g[l].partition_broadcast(128), w=["gv_bc"])
        P.dma("sp", ws_f, sgu_w[l].rearrange("h q p -> q h p"), w=["ws_f"])
        for hg in range(4):
            P.dma("sp", bsb4[:, hg, :], sgu_b[l, hg].partition_broadcast(128), w=["bsb4"])
        P.op("dve", v_cp(ws_b, ws_f), r=["ws_f"], w=["ws_b"])
        for hg in range(4):
            pt = banks[7].bitcast(BF16)[:, 0:128]
            P.op("pe", p_tr(pt, ws_b[:, hg, :], ident), r=["ws_b", "ident"], w=[BK(7)])
            P.op("dve", v_cp(wsT[:, hg, :], pt), r=[BK(7)], w=["wsT"])

        wv = wb["in"][l].rearrange("(kc p) c -> p kc c", p=128)
        slab_seq = [(b, s) for b in range(NB) for s in SLAB_ORDER]
        wn = {"n": 0}

        def prefetch(upto):
            while wn["n"] <= upto and wn["n"] < len(slab_seq):
                i = wn["n"]
                wn["n"] += 1
                k = i % NSL
                s = slab_seq[i][1]
                P.dma("sp", wsl[k], wv[:, :, s * 512:(s + 1) * 512], r=[("wb", "in", l, s)], w=[("wsl", k)])

        xn = {"n": 0}

        def stage_A(b):
            si, t0, _, _ = blocks[b]
            hb = b % 2
            for tt in range(4):
                i = xn["n"]
                xn["n"] += 1
                k = 0
                sc = ssq[:, (i % 8):(i % 8) + 1]
                P.dma("sp", xt[k], xsrc[t0 + tt * 128:t0 + (tt + 1) * 128, :], w=[("xt", k)])
                P.op("act", lambda e, k=k, sc=sc: e.activation(out=ht[k], in_=xt[k], func=AF.Square, accum_out=sc),
                     r=[("xt", k)], w=[("ht", k), "ssq"], emb=False)
                P.op("act", a_act(sc, sc, AF.Sqrt, scale=1.0 / D, bias=EPS), r=["ssq"], w=["ssq"])
                P.op("dve", lambda e, sc=sc: e.reciprocal(out=sc, in_=sc), r=["ssq"], w=["ssq"])
                P.op("dve", v_stt(ht[k], xt[k], sc, g1bc, ALU.mult, ALU.mult), r=[("xt", k), "ssq", "g1bc"], w=[("ht", k)])
                for g4 in range(4):
                    bi = 6 + (g4 % 2)
                    pt = banks[bi].bitcast(BF16)
                    for q in range(4):
                        fc = g4 * 4 + q
                        P.op("pe", p_tr(pt[:, q * 128:(q + 1) * 128], ht[k][:, fc * 128:(fc + 1) * 128], ident),
                             r=[("ht", k), "ident"], w=[BK(bi)], signal=(q == 3))
                    P.op("act", a_act(hT[hb][:, g4 * 4:(g4 + 1) * 4, tt * 128:(tt + 1) * 128],
                                      pt[:, 0:512].rearrange("p (a b) -> p a b", a=4), AF.Copy),
                         r=[BK(bi)], w=[("hT", hb)])
            P.dma("sp", rc[hb], ropec[:, t0:t0 + 512], w=[("rc", hb)])
            P.dma("sp", rs[hb], ropes[:, t0:t0 + 512], w=[("rs", hb)])

        accn = {"n": 0}
        cnt = {"z": 0, "ab": 0, "q": 0, "v": 0, "oc": 0, "qk": 0}

        def proj(b, k, fm, idx):
            hb = b % 2
            a = accn["n"] % 3
            accn["n"] += 1
            acc = banks[a]
            for kc in range(16):
                if fm:
                    f = p_mm(acc, wsl[k][:, kc, idx * 128:(idx + 1) * 128], hT[hb][:, kc, :], kc == 0, kc == 15)
                else:
                    f = p_mm(acc, hT[hb][:, kc, idx * 128:(idx + 1) * 128], wsl[k][:, kc, :], kc == 0, kc == 15)
                P.op("pe", f, r=[("wsl", k), ("hT", hb)], w=[BK(a)], signal=(kc == 15))
            return acc, BK(a)

        def qk_post(b, which, h, acc, ak):
            hb = b % 2
            t0 = blocks[b][1]
            w = cnt["qk"] % 2
            cnt["qk"] += 1
            rgm, rgk = (rg_q, "rg_q") if which == "q" else (rg_k, "rg_k")
            g8 = col(C_G8Q) if which == "q" else col(C_G8K)
            P.op("act", a_act(sqb[w], acc, AF.Square), r=[ak], w=[("sqb", w)])
            P.op("act", a_act(qb16[w], acc, AF.Copy), r=[ak], w=[("qb16", w)])
            return lambda: qk_post_b(b, which, h, acc, ak, w, rgm, rgk, g8)

        def qk_post_b(b, which, h, acc, ak, w, rgm, rgk, g8):
            hb = b % 2
            t0 = blocks[b][1]
            P.op("pe", p_mm(banks[3], blk, sqb[w]), r=["blk", ("sqb", w)], w=[BK(3)])
            P.op("pe", p_mm(banks[4], rgm, qb16[w]), r=[rgk, ("qb16", w)], w=[BK(4)])
            P.op("act", a_act(rstd[w], banks[3], AF.Sqrt, scale=1.0 / 64, bias=EPS), r=[BK(3)], w=[("rstd", w)])
            P.op("dve", lambda e, w=w: e.reciprocal(out=rstd[w], in_=rstd[w]), r=[("rstd", w)], w=[("rstd", w)])
            P.op("dve", v_stt(qn[w], acc, g8, rstd[w], ALU.mult, ALU.mult), r=[ak, "cols", ("rstd", w)], w=[("qn", w)])
            P.op("dve", v_tt(t2[w], banks[4], rstd[w], ALU.mult), r=[BK(4), ("rstd", w)], w=[("t2", w)])
            P.op("dve", v_tt(qn[w], qn[w], rc[hb], ALU.mult), r=[("qn", w), ("rc", hb)], w=[("qn", w)])
            P.op("dve", v_tt(t2[w], t2[w], rs[hb], ALU.mult), r=[("t2", w), ("rs", hb)], w=[("t2", w)])
            o = cnt["q"] % 3
            cnt["q"] += 1
            P.op("dve", v_tt(qo[o], qn[w], t2[w], ALU.add), r=[("qn", w), ("t2", w)], w=[("qo", o)])
            si, c0 = blocks[b][0], segs[blocks[b][0]][0]
            dst = qT[h, :, t0:t0 + 512] if which == "q" else kT_own[h][si][:, t0 - c0:t0 - c0 + 512]
            P.dma("pool", dst, qo[o], r=[("qo", o)])

        stage_A(0)
        prefetch(1)
        si_ = 0
        qk_pend = []
        for b in range(NB):
            si, t0, first, last = blocks[b]
            hb = b % 2
            segbase = 0 if si == 0 else T0 + 2
            segc0 = segs[si][0]
            for s in SLAB_ORDER:
                k = si_ % NSL
                prefetch(si_ + 2)
                si_ += 1
                issue_conv(("rest", l), 1)
                if s == 0:
                    for m in range(4):
                        acc, ak = proj(b, k, True, m)
                        P.op("act", a_act(ax[:, m, :], acc, AF.Copy), r=[ak], w=[("ax", m)])
                elif s == 2:
                    for m in range(4):
                        acc, ak = proj(b, k, True, m)
                        o = cnt["z"] % 2
                        cnt["z"] += 1
                        P.op("dve", v_tt(zst[o], acc, ax[:, m, :], ALU.mult), r=[ak, ("ax", m)], w=[("zst", o)])
                        c = segbase + 1 + (t0 - segc0)
                        P.dma("pool", zT[m * 128:(m + 1) * 128, c:c + 512], zst[o], r=[("zst", o)])
                        if first:
                            P.dma("pool", zb_own[2 * si, m * 128:(m + 1) * 128].rearrange("(d o) -> d o", o=1), zst[o][:, 0:1], r=[("zst", o)])
                        if last:
                            P.dma("pool", zb_own[2 * si + 1, m * 128:(m + 1) * 128].rearrange("(d o) -> d o", o=1), zst[o][:, 511:512], r=[("zst", o)])
                elif s == 1:
                    for m in range(4):
                        acc, ak = proj(b, k, True, m)
                        o = cnt["ab"] % 2
                        cnt["ab"] += 1
                        P.op("act", a_act(abst[o], acc, AF.Copy), r=[ak], w=[("abst", o)])
                        P.dma("pool", abT[m * 128:(m + 1) * 128, t0:t0 + 512], abst[o], r=[("abst", o)])
                elif s in (3, 4, 5, 6):
                    which = "q" if s < 5 else "k"
                    for m in range(4):
                        h = ((s - 3) % 2) * 4 + m
                        acc, ak = proj(b, k, True, m)
                        if qk_pend:
                            qk_pend.pop(0)()
                        qk_pend.append(qk_post(b, which, h, acc, ak))
                    if s == 6:
                        while qk_pend:
                            qk_pend.pop(0)()
                elif s in (7, 8):
                    for tt in range(4):
                        acc, ak = proj(b, k, False, tt)
                        o = cnt["v"] % 3
                        cnt["v"] += 1
                        P.op("act", a_act(vst[o], acc, AF.Copy), r=[ak], w=[("vst", o)])
                        kt = (t0 - segc0 + tt * 128) // 128
                        for hh in range(4):
                            h = (s - 7) * 4 + hh
                            P.dma("pool", v_own[h][si][:, kt * 128:(kt + 1) * 128], vst[o][:, hh * 128:(hh + 1) * 128], r=[("vst", o)])
                elif s == 9:
                    for m in range(4):
                        acc, ak = proj(b, k, True, m)
                        gelu(acc, ak, ug[:, m, :], ("ug", m), G)
                elif s == 10:
                    for tt in range(4):
                        acc, ak = proj(b, k, False, tt)
                        gelu(acc, ak, vg, "vg", G)
                        P.op("dve", v_tt(vtmp, vg, vg, ALU.mult), r=["vg"], w=["octmp"])
                        P.op("dve", v_red(vss, vtmp.rearrange("p (a b) -> p a b", a=4), ALU.add), r=["octmp"], w=["vss"])
                        P.op("act", a_act(vss, vss, AF.Sqrt, scale=1.0 / 128, bias=EPS), r=["vss"], w=["vss"])
                        P.op("dve", lambda e: e.reciprocal(out=vss, in_=vss), r=["vss"], w=["vss"])
                        for hg in range(4):
                            P.op("dve", v_stt(vn[:, tt, hg * 128:(hg + 1) * 128], vg[:, hg * 128:(hg + 1) * 128], vss[:, hg:hg + 1], gv_bc,
                                              ALU.mult, ALU.mult), r=["vg", "vss", "gv_bc"], w=[("vn", tt)])
                    for hg in range(4):
                        for tt in range(4):
                            P.op("pe", p_mm(banks[5][:, tt * 128:(tt + 1) * 128], vn[:, tt, hg * 128:(hg + 1) * 128], wsT[:, hg, :]),
                                 r=[("vn", tt), "wsT"], w=[BK(5)], signal=(tt == 3))
                        for tt in range(4):
                            P.op("dve", v_tt(octmp[:, tt * 128:(tt + 1) * 128], banks[5][:, tt * 128:(tt + 1) * 128], bsb4[:, hg, :], ALU.add),
                                 r=[BK(5), "bsb4"], w=["octmp"])
                        o = cnt["oc"] % 2
                        cnt["oc"] += 1
                        P.op("dve", v_tt(outc[o], octmp, ug[:, hg, :], ALU.mult), r=["octmp", ("ug", hg)], w=[("outc", o)])
                        P.dma("pool", mixT[1536 + hg * 128:1536 + (hg + 1) * 128, t0:t0 + 512], outc[o], r=[("outc", o)])
                if s == 5 and b + 1 < NB:
                    stage_A(b + 1)

    def gather():
        ccs = P.new_sem("cc")
        n = 0
        pairs = [(zb_own, zb_g)]
        for si in range(2):
            for h in range(NH):
                pairs.append((kT_own[h][si], kT_g[h][si]))
                pairs.append((v_own[h][si], v_g[h][si]))
        for src, dst in pairs:
            n += 1
            P.q["pool"].append(lambda e, src=src, dst=dst: e.collective_compute(
                "AllGather", ALU.bypass, replica_groups=GROUPS, ins=[src.ap().opt()], outs=[dst.ap().opt()]).then_inc(ccs))
        P.q["pool"].append(lambda e: e.wait_ge(ccs, n))
        P.op("pool", lambda e: e.memset(cols[:, 31:32], 0.0), w=["ccdone"])
        P.barrier()
        P.dma("sp", zbg_sb, zb_g.ap(), w=["zbg_sb"])
        for c in range(4):
            P.op("pe", p_mm(banks[7][:, 0:4], zbg_sb[:, c * 128:(c + 1) * 128], sel), r=["zbg_sb", "sel"], w=[BK(7)])
            P.op("dve", v_cp(halo[:, c, :], banks[7][:, 0:4]), r=[BK(7)], w=["halo"])

    def pass2(l):
        A = Arena(arena_t, PASS_BASE)
        SMAX = 4 * T1
        kTs = [A.alloc([128, SMAX], BF16) for _ in range(2)]
        Vs = [A.alloc([128, SMAX // 128, 128], BF16) for _ in range(2)]
        qs = [A.alloc([128, 512], BF16) for _ in range(2)]
        NE = 6
        Es_all = A.alloc([128, NE, 1024], BF16)
        Es = [Es_all[:, k, :] for k in range(NE)]
        zacc = [[A.alloc([128, 2, 1024], F32) for _ in range(2)] for _ in range(2)]
        ones_f = A.alloc([128, 128], F32)
        rz = A.alloc([128, 512], F32)
        o1 = A.alloc([128, 512], F32)
        o2 = A.alloc([128, 512], F32)
        osq = A.alloc([128, 512], BF16)
        rstd = A.alloc([128, 512], F32)
        obs = [A.alloc([128, 512], BF16) for _ in range(2)]
        Sb = [psA, psB, None]
        NS = 3
        Sv = [psA[:, :], psB[:, :]]
        U1, U2 = banks[6], banks[7]
        P.op("dve", lambda e: e.memset(ones_f, 1.0), w=["ones_f"])

        groups = []
        for si, (c0, tl) in enumerate(segs):
            for h in range(NH):
                for qb in range(tl // 512):
                    groups.append((si, h, qb))
        heads = []
        for g in groups:
            if not heads or heads[-1] != (g[0], g[1]):
                heads.append((g[0], g[1]))
        head_idx = {hd: i for i, hd in enumerate(heads)}

        def load_head(hi):
            si, h = heads[hi]
            c0, tl = segs[si]
            hb = hi % 2
            nkt = tl // 128
            for r in range(4):
                P.dma("sp", kTs[hb][:, r * tl:(r + 1) * tl], kT_g[h][si][r * 128:(r + 1) * 128, :], w=[("kTs", hb)])
                P.dma("sp", Vs[hb][:, r * nkt:(r + 1) * nkt, :], v_g[h][si][r * 128:(r + 1) * 128, :],
                      w=[("Vs", hb)], key=("Vsl", hb))

        iters = []
        for gi, (si, h, qb) in enumerate(groups):
            nk = 4 * segs[si][1] // 128
            for kt in range(nk):
                iters.append((gi, kt, nk))

        sctr = {"n": 0}
        s_of = {}
        e_assume = {}
        za_last = {}

        def s_halves(j):
            if j < 2:
                t = Sv[j]
                return t[:, 0:512], t[:, 512:1024], t
            return banks[4], banks[5], None

        def emit_S(i):
            gi, kt, nk = iters[i]
            si, h, qb = groups[gi]
            hb = head_idx[(si, h)] % 2
            if kt == 0:
                c0, tl = segs[si]
                P.dma("sp", qs[gi % 2], qT[h, :, c0 + qb * 512:c0 + (qb + 1) * 512], w=[("qs", gi % 2)])
            j = sctr["n"] % 2
            sctr["n"] += 1
            s_of[i] = j
            etoks = P.war_tokens([("E", i % NE)])
            P.wait_toks("pe", [t for t in etoks if P.owner.get(id(t[0])) in ("dve", "pool")])
            e_assume[i] = etoks
            lo, hi, _ = s_halves(j)
            P.need("pe", [("kTs", hb), ("qs", gi % 2)])
            P.op("pe", p_mm(lo, kTs[hb][0:64, kt * 128:(kt + 1) * 128], qs[gi % 2][0:64, :], True, True, (0, 0)),
                 r=[("kTs", hb), ("qs", gi % 2)], w=[("S", j)], signal=False, emb=True)
            P.op("pe", p_mm(hi, kTs[hb][64:128, kt * 128:(kt + 1) * 128], qs[gi % 2][64:128, :], True, True, (64, 0)),
                 r=[("kTs", hb), ("qs", gi % 2)], w=[("S", j)], emb=True)

        def emit_exp(i):
            ek = i % NE
            j = s_of.pop(i)
            P.assume("act", e_assume.pop(i))
            P.op("act", a_act(Es[ek], s_halves(j)[2], AF.Exp, scale=0.125, bias=col(C_NEGC)), r=[("S", j), "cols"], w=[("E", ek)])

        def emit_rest(i):
            gi, kt, nk = iters[i]
            si, h, qb = groups[gi]
            hb = head_idx[(si, h)] % 2
            ek = i % NE
            st, sp_ = (kt == 0), (kt == nk - 1)
            P.need("pe", [("Vs", hb)])
            P.op("pe", p_mm(U1, Vs[hb][:, kt, :], Es[ek][:, 0:512], st, sp_), r=[("Vs", hb), ("E", ek)], w=["U1"], signal=False, emb=True)
            P.op("pe", p_mm(U2, Vs[hb][:, kt, :], Es[ek][:, 512:1024], st, sp_), r=[("Vs", hb), ("E", ek)], w=["U2"], emb=True)
            if kt % 2 == 1:
                par = (kt // 2) % 2
                za = zacc[gi % 2][par]
                zk = ("zacc", gi % 2, par)
                epair = Es_all[:, ek - 1:ek + 1, :]
                if kt // 2 < 2:
                    P.op("dve", v_cp(za, epair), r=[("E", ek - 1), ("E", ek)], w=[zk])
                else:
                    P.assume("dve", [P.lw[zk]])
                    P.op("dve", v_tt(za, za, epair, ALU.add), r=[("E", ek - 1), ("E", ek), zk], w=[zk])
            if sp_:
                epilogue(gi)

        def epilogue(gi):
            si, h, qb = groups[gi]
            c0, tl = segs[si]
            za0, za1 = zacc[gi % 2]
            zk0, zk1 = ("zacc", gi % 2, 0), ("zacc", gi % 2, 1)
            AUXA, AUXB = banks[4], banks[5]
            parts = [(za0, 0), (za0, 1), (za1, 0), (za1, 1)]
            for n_, (zz, sl) in enumerate(parts):
                P.op("pe", p_mm(AUXA, ones_f, zz[:, sl, 0:512], n_ == 0, n_ == 3), r=["ones_f", zk0, zk1], w=[BK(4)], signal=(n_ == 3))
            for n_, (zz, sl) in enumerate(parts):
                P.op("pe", p_mm(AUXB, ones_f, zz[:, sl, 512:1024], n_ == 0, n_ == 3), r=["ones_f", zk0, zk1], w=[BK(5)], signal=(n_ == 3))
            P.op("dve", lambda e: e.reciprocal(out=rz, in_=AUXA), r=[BK(4)], w=["rz"])
            P.op("dve", v_tt(o1, U1, rz, ALU.mult), r=["U1", "rz"], w=["o1"])
            P.op("dve", lambda e: e.reciprocal(out=rz, in_=AUXB), r=[BK(5)], w=["rz"])
            P.op("dve", v_stt(o2, U2, col(C_NEGLAM), rz, ALU.mult, ALU.mult), r=["U2", "rz", "cols"], w=["o2"])
            P.op("dve", v_tt(o1, o1, o2, ALU.add), r=["o1", "o2"], w=["o1"])
            P.op("dve", v_tt(osq, o1, o1, ALU.mult), r=["o1"], w=["osq"])
            P.op("pe", p_mm(AUXA, ones, osq), r=["ones", "osq"], w=[BK(4)])
            P.op("act", a_act(rstd, AUXA, AF.Ln, scale=1.0 / 128, bias=EPS), r=[BK(4)], w=["rstd2"])
            P.op("act", a_act(rstd, rstd, AF.Exp, scale=-0.5), r=["rstd2"], w=["rstd2"])
            ob = gi % 2
            P.op("dve", v_stt(obs[ob], o1, col(C_GSUB), rstd, ALU.mult, ALU.mult), r=["o1", "rstd2", "cols"], w=[("obs", ob)])
            P.dma("pool", mixT[512 + h * 128:512 + (h + 1) * 128, c0 + qb * 512:c0 + (qb + 1) * 512], obs[ob], r=[("obs", ob)])

        load_head(0)
        if len(heads) > 1:
            load_head(1)
        loaded = 2
        emit_S(0)
        emit_S(1)
        for i in range(len(iters)):
            emit_exp(i)
            if i + 2 < len(iters):
                emit_S(i + 2)
            emit_rest(i)
            gi, kt, nk = iters[i]
            if kt == nk - 1 and (gi + 1 == len(groups) or groups[gi + 1][:2] != groups[gi][:2]):
                if loaded < len(heads):
                    load_head(loaded)
                    loaded += 1

    def pass3(l, xsrc, xdst):
        A = Arena(arena_t, PASS_BASE)
        g2bc = A.alloc([128, D], F32)
        xb = A.alloc([128, 4, D], F32)
        mix = [A.alloc([128, 16, 512], BF16) for _ in range(1)]
        h2 = A.alloc([128, D], BF16)
        h2T = mix[0]
        act = A.alloc([128, 64, 512], BF16)
        NSL = 3
        wsl = [A.alloc([128, 16, 512], BF16) for _ in range(NSL)]
        zw = [A.alloc([128, 514], F32) for _ in range(2)]
        abw = [A.alloc([128, 512], F32) for _ in range(2)]
        cacc = A.alloc([128, 512], F32)
        rl = [A.alloc([128, 512], F32) for _ in range(2)]
        ssq = A.alloc([128, 8], F32)

        P.dma("sp", g2bc, norm2_g[l].partition_broadcast(128), w=["g2bc"])
        wo = wb["out"][l].rearrange("(kc p) c -> p kc c", p=128)
        wu = wb["up"][l].rearrange("(kc p) c -> p kc c", p=128)
        wd = wb["down"][l].rearrange("(kc p) c -> p kc c", p=128)
        seq = []
        for b in range(NB):
            seq += [("out", n) for n in range(4)] + [("up", s) for s in range(16)] + [("down", n, s) for n in range(4) for s in range(4)]
        wn = {"n": 0}

        def prefetch(upto):
            while wn["n"] <= upto and wn["n"] < len(seq):
                i = wn["n"]
                wn["n"] += 1
                k = i % NSL
                it = seq[i]
                if it[0] == "out":
                    src, key = wo[:, :, it[1] * 512:(it[1] + 1) * 512], ("wb", "out", l, it[1])
                elif it[0] == "up":
                    src, key = wu[:, :, it[1] * 512:(it[1] + 1) * 512], ("wb", "up", l, it[1])
                else:
                    src, key = wd[:, it[2] * 16:(it[2] + 1) * 16, it[1] * 512:(it[1] + 1) * 512], ("wb", "down", l, it[1], it[2])
                P.dma("sp", wsl[k], src, r=[key], w=[("wsl", k)])

        wi = 0
        issue_conv(("rest", l))
        prefetch(1)
        accn = 0
        cn = 0
        for b in range(NB):
            si, t0, first, last = blocks[b]
            segbase = 0 if si == 0 else T0 + 2
            segc0 = segs[si][0]
            mx = mix[0]
            for tt in range(4):
                P.dma("sp", xb[:, tt, :], xsrc[t0 + tt * 128:t0 + (tt + 1) * 128, :], w=[("xb", tt)])
            P.dma("sp", mx[:, 4:16, :], mixT[512:2048, t0:t0 + 512].rearrange("(c p) t -> p c t", p=128), w=["mix"])
            for c in range(4):
                o = cn % 2
                cn += 1
                cz = segbase + (t0 - segc0)
                P.dma("sp", zw[o], zT[c * 128:(c + 1) * 128, cz:cz + 514], w=[("zw", o)])
                P.dma("sp", abw[o], abT[c * 128:(c + 1) * 128, t0:t0 + 512], w=[("abw", o)])
                if first:
                    P.op("dve", v_cp(zw[o][:, 0:1], halo[:, c, 2 * si:2 * si + 1]), r=["halo"], w=[("zw", o)])
                if last:
                    P.op("dve", v_cp(zw[o][:, 513:514], halo[:, c, 2 * si + 1:2 * si + 2]), r=["halo"], w=[("zw", o)])
                P.op("dve", v_ts(cacc, zw[o][:, 1:513], convw[:, c, 1:2], None, ALU.mult), r=[("zw", o), "convw"], w=["cacc"])
                P.op("dve", v_stt(cacc, zw[o][:, 0:512], convw[:, c, 0:1], cacc, ALU.mult, ALU.add), r=[("zw", o), "convw", "cacc"], w=["cacc"])
                P.op("dve", v_stt(cacc, zw[o][:, 2:514], convw[:, c, 2:3], cacc, ALU.mult, ALU.add), r=[("zw", o), "convw", "cacc"], w=["cacc"])
                P.op("dve", v_tt(mx[:, c, :], cacc, abw[o], ALU.mult), r=["cacc", ("abw", o)], w=["mix"])
            for n in range(4):
                k = wi % NSL
                prefetch(wi + 2)
                wi += 1
                for tt in range(4):
                    a = accn % 4
                    accn += 1
                    for kc in range(16):
                        P.op("pe", p_mm(banks[a], mx[:, kc, tt * 128:(tt + 1) * 128], wsl[k][:, kc, :], kc == 0, kc == 15),
                             r=["mix", ("wsl", k)], w=[BK(a)], signal=(kc == 15))
                    P.op("dve", v_tt(xb[:, tt, n * 512:(n + 1) * 512], banks[a], xb[:, tt, n * 512:(n + 1) * 512], ALU.add),
                         r=[BK(a), ("xb", tt)], w=[("xb", tt)])
            for tt in range(4):
                sc = ssq[:, tt:tt + 1]
                P.op("act", lambda e, tt=tt, sc=sc: e.activation(out=h2, in_=xb[:, tt, :], func=AF.Square, accum_out=sc),
                     r=[("xb", tt)], w=["h2", "ssq3"], emb=False)
                P.op("act", a_act(sc, sc, AF.Sqrt, scale=1.0 / D, bias=EPS), r=["ssq3"], w=["ssq3"])
                P.op("dve", lambda e, sc=sc: e.reciprocal(out=sc, in_=sc), r=["ssq3"], w=["ssq3"])
                P.op("dve", v_stt(h2, xb[:, tt, :], sc, g2bc, ALU.mult, ALU.mult), r=[("xb", tt), "ssq3", "g2bc"], w=["h2"])
                for g4 in range(4):
                    bi = 6 + (g4 % 2)
                    pt = banks[bi].bitcast(BF16)
                    for q in range(4):
                        fc = g4 * 4 + q
                        P.op("pe", p_tr(pt[:, q * 128:(q + 1) * 128], h2[:, fc * 128:(fc + 1) * 128], ident),
                             r=["h2", "ident"], w=[BK(bi)], signal=(q == 3))
                    P.op("act", a_act(h2T[:, g4 * 4:(g4 + 1) * 4, tt * 128:(tt + 1) * 128],
                                      pt[:, 0:512].rearrange("p (a b) -> p a b", a=4), AF.Copy), r=[BK(bi)], w=["mix"])
            for s in range(16):
                k = wi % NSL
                prefetch(wi + 2)
                wi += 1
                issue_conv(("in", l + 1), 1)
                for m in range(4):
                    a = accn % 4
                    accn += 1
                    for kc in range(16):
                        P.op("pe", p_mm(banks[a], wsl[k][:, kc, m * 128:(m + 1) * 128], h2T[:, kc, :], kc == 0, kc == 15),
                             r=["mix", ("wsl", k)], w=[BK(a)], signal=(kc == 15))
                    ro = accn % 2
                    P.op("act", a_act(rl[ro], banks[a], AF.Relu), r=[BK(a)], w=[("rl", ro)])
                    P.op("pool", v_tt(act[:, s * 4 + m, :], rl[ro], rl[ro], ALU.mult), r=[("rl", ro)], w=["act"])
            for n in range(4):
                base = 0 if n % 2 == 0 else 4
                for s in range(4):
                    k = wi % NSL
                    prefetch(wi + 2)
                    wi += 1
                    for fc in range(16):
                        for tt in range(4):
                            lastmm = (s == 3 and fc == 15)
                            P.op("pe", p_mm(banks[base + tt], act[:, s * 16 + fc, tt * 128:(tt + 1) * 128], wsl[k][:, fc, :],
                                            s == 0 and fc == 0, lastmm),
                                 r=["act", ("wsl", k)], w=[BK(base + tt)], signal=(lastmm or (fc == 15 and tt == 3)))
                for tt in range(4):
                    P.op("dve", v_tt(xb[:, tt, n * 512:(n + 1) * 512], banks[base + tt], xb[:, tt, n * 512:(n + 1) * 512], ALU.add),
                         r=[BK(base + tt), ("xb", tt)], w=[("xb", tt)])
            for tt in range(4):
                P.dma("pool", xdst[t0 + tt * 128:t0 + (tt + 1) * 128, :], xb[:, tt, :], r=[("xb", tt)])

    for l in range(depth):
        xsrc = x_in if l == 0 else x1
        xdst = y_out if l == depth - 1 else x1
        layer_consts(l)
        issue_conv(("in", l))
        pass1(l, xsrc)
        P.barrier()
        if stop_after == ("p1", l):
            break
        gather()
        if stop_after == ("g", l):
            break
        pass2(l)
        P.barrier()
        if stop_after == ("p2", l):
            break
        pass3(l, xsrc, xdst)
        P.barrier()

    with nc.Block() as block:
        @block.tensor
        def _(e):
            for f in P.q["pe"]:
                f(e)

        @block.scalar
        def _(e):
            for f in P.q["act"]:
                f(e)

        @block.vector
        def _(e):
            for f in P.q["dve"]:
                f(e)

        @block.gpsimd
        def _(e):
            for f in P.q["pool"]:
                f(e)

        @block.sync
        def _(e):
            for f in P.q["sp"]:
                f(e)
    stack.close()
    print("instr counts", {k: len(v) for k, v in P.q.items()}, "nsem", P.nsem, flush=True)
    return nc


def host_consts(T0, T1, core):
    r = core % 4
    pos = np.concatenate([r * T0 + np.arange(T0), r * T1 + np.arange(T1)]).astype(np.float32)
    inv = (ROPE_THETA ** (-np.arange(0, 16, 2, dtype=np.float32) / 16)).astype(np.float32)
    ang = pos[:, None] * inv[None, :]
    cs, sn = np.cos(ang).astype(np.float32), np.sin(ang).astype(np.float32)
    T = T0 + T1
    C = np.ones((128, T), np.float32)
    S = np.zeros((128, T), np.float32)
    for gb in (0, 64):
        for d in range(16):
            C[gb + d] = cs[:, d % 8]
            S[gb + d] = sn[:, d % 8]
    ident = np.eye(128, dtype=np.float32)
    blk = np.zeros((128, 128), np.float32)
    blk[:64, :64] = 1
    blk[64:, 64:] = 1
    rmat = np.zeros((128, 128), np.float32)
    for gb in (0, 64):
        for m in range(8):
            rmat[gb + m + 8, gb + m] = -1.0
        for m in range(8, 16):
            rmat[gb + m - 8, gb + m] = 1.0
    sel = np.zeros((16, 4), np.float32)
    for s in range(2):
        if r > 0:
            sel[(r - 1) * 4 + 2 * s + 1, 2 * s] = 1.0
        if r < 3:
            sel[(r + 1) * 4 + 2 * s, 2 * s + 1] = 1.0
    selz = np.zeros((64, 256), np.float32)
    selz[0, 0:128] = 1.0
    selz[32, 128:256] = 1.0
    return {"ropec": C, "ropes": S, "c_ident": ident, "c_blk": blk, "c_rmat": rmat, "c_sel": sel, "c_selz": selz}


_CACHE = {}


def run(inputs, T0, T1, depth=2, stop_after=None, trace=False):
    key = (T0, T1, depth, stop_after)
    if key not in _CACHE:
        _CACHE[key] = build(T0, T1, depth, stop_after)
    nc = _CACHE[key]
    xp = np.asarray(inputs["x_prompt"], np.float32)
    xs = np.asarray(inputs["x_sample"], np.float32)
    shared = {k: np.ascontiguousarray(np.asarray(inputs[k], np.float32)) for k in (
        "w_in", "w_out", "w_up", "w_down", "norm1_g", "norm2_g", "conv_w", "q_norm_g", "k_norm_g",
        "lam_q1", "lam_k1", "lam_q2", "lam_k2", "subln_g", "sgu_norm_g", "sgu_w", "sgu_b")}
    in_maps = []
    for c in range(NCORES):
        g, r = c // 4, c % 4
        m = dict(shared)
        m["x_in"] = np.ascontiguousarray(np.concatenate([xp[g, r * T0:(r + 1) * T0], xs[g, r * T1:(r + 1) * T1]], 0))
        m.update(host_consts(T0, T1, c))
        in_maps.append(m)
    res = run_bass_kernel_spmd(nc, in_maps, core_ids=list(range(NCORES)), **({"trace": True} if trace else {}))
    yp = np.zeros((2, 4 * T0, D), np.float32)
    ys = np.zeros((2, 4 * T1, D), np.float32)
    for c in range(NCORES):
        g, r = c // 4, c % 4
        y = np.asarray(res.results[c]["y_out"], np.float32)
        yp[g, r * T0:(r + 1) * T0] = y[:T0]
        ys[g, r * T1:(r + 1) * T1] = y[T0:]
    return (yp, ys), res


def kernel(**inputs):
    (yp, ys), _ = run(inputs, 1024, 4096, 2)
    return (yp, ys)
```
